# Optimizing a Trainium2 kernel written in Bass

```python
import math
import jax, jax.numpy as jnp
from jax import lax
import numpy as np

D_MODEL = 1024
BATCH = 2
SEQ = 8192
DEPTH = 1

GRID_W = 64
HEAD_DIM = 64
NA_HEADS = 8
NA_WIDTH = NA_HEADS * HEAD_DIM
NA_KR_MAX = 8
NA_KC = 16
DA_HEADS = 4
DA_VDIM = 2 * HEAD_DIM
DA_QK = DA_HEADS * 2 * HEAD_DIM
DA_WIDTH = DA_HEADS * DA_VDIM
MIX_WIDTH = NA_WIDTH + DA_WIDTH
IN_COLS = 3 * NA_WIDTH + 2 * DA_QK + DA_WIDTH
D_FF = -(-8 * D_MODEL // (3 * 256)) * 256
ROPE_THETA = 10000.0
LN_EPS = 1e-5
BLOCK_Q = 128
ALPHA = (2.0 * DEPTH) ** 0.25
BETA = (8.0 * DEPTH) ** -0.25

kernel_name = "hybrid_natten_diffattn_deepnorm_adaln_encoder"


def lambda_init_for(layer_idx):
    return 0.8 - 0.6 * math.exp(-0.3 * (layer_idx - 1))


def layer_norm(x, g=None, b=None):
    xf = x.astype(jnp.float32)
    mu = jnp.mean(xf, axis=-1, keepdims=True)
    var = jnp.mean(jnp.square(xf - mu), axis=-1, keepdims=True)
    y = (xf - mu) * lax.rsqrt(var + LN_EPS)
    if g is not None:
        y = y * g.astype(jnp.float32) + b.astype(jnp.float32)
    return y.astype(x.dtype)


def rms_norm(x, g):
    xf = x.astype(jnp.float32)
    y = xf * lax.rsqrt(jnp.mean(jnp.square(xf), axis=-1, keepdims=True) + LN_EPS)
    return (y * g.astype(jnp.float32)).astype(x.dtype)


def rope(x):
    S_ = x.shape[1]
    half = HEAD_DIM // 2
    freqs = 1.0 / (ROPE_THETA ** (jnp.arange(half, dtype=jnp.float32) / half))
    ang = jnp.arange(S_, dtype=jnp.float32)[:, None] * freqs[None, :]
    cos = jnp.cos(ang)[None, :, None, None, :]
    sin = jnp.sin(ang)[None, :, None, None, :]
    xf = x.astype(jnp.float32)
    x1, x2 = xf[..., :half], xf[..., half:]
    return jnp.concatenate([x1 * cos - x2 * sin, x2 * cos + x1 * sin], axis=-1).astype(x.dtype)


def neighbourhood_attention(q, k, v, rpb):
    B_, S_ = q.shape[0], q.shape[1]
    rows = S_ // GRID_W
    kr = min(NA_KR_MAX, rows)
    scale = HEAD_DIM ** -0.5

    def to_grid(t):
        return t.reshape(B_, rows, GRID_W, NA_HEADS, HEAD_DIM).transpose(1, 0, 3, 2, 4)

    q_g, k_g, v_g = to_grid(q), to_grid(k), to_grid(v)
    col = np.arange(GRID_W)
    col_start = np.clip(col - NA_KC // 2, 0, GRID_W - NA_KC)
    col_idx = col_start[:, None] + np.arange(NA_KC)[None, :]
    col_rel = col_idx - col[:, None] + (NA_KC - 1)
    rpb_cols = rpb[:, :, col_rel]

    def one_row(args):
        q_row, r = args
        rs = jnp.clip(r - kr // 2, 0, rows - kr)
        k_rows = lax.dynamic_slice_in_dim(k_g, rs, kr, axis=0)
        v_rows = lax.dynamic_slice_in_dim(v_g, rs, kr, axis=0)
        k_win = k_rows[:, :, :, col_idx]
        v_win = v_rows[:, :, :, col_idx]
        row_rel = rs + jnp.arange(kr) - r + (NA_KR_MAX - 1)
        bias = rpb_cols[:, row_rel].transpose(0, 2, 1, 3)
        s = jnp.einsum('bhwd,rbhwkd->bhwrk', q_row, k_win).astype(jnp.float32) * scale
        s = s + bias[None].astype(jnp.float32)
        p = jax.nn.softmax(s.reshape(B_, NA_HEADS, GRID_W, kr * NA_KC), axis=-1)
        p = p.reshape(B_, NA_HEADS, GRID_W, kr, NA_KC).astype(v.dtype)
        return jnp.einsum('bhwrk,rbhwkd->bhwd', p, v_win)

    out = lax.map(one_row, (q_g, jnp.arange(rows)))
    return out.transpose(1, 0, 3, 2, 4).reshape(B_, S_, NA_WIDTH)


def differential_attention(q, k, v, lam, subln_g, lambda_init):
    B_, S_ = q.shape[0], q.shape[1]
    nblk = S_ // BLOCK_Q
    scale = HEAD_DIM ** -0.5
    qb = q.reshape(B_, nblk, BLOCK_Q, DA_HEADS, 2, HEAD_DIM).transpose(1, 0, 3, 4, 2, 5)
    kt = k.transpose(0, 2, 3, 1, 4)
    vt = v.transpose(0, 2, 1, 3)

    def one_block(q_blk):
        s = jnp.einsum('bhiqd,bhikd->bhiqk', q_blk, kt).astype(jnp.float32) * scale
        p = jax.nn.softmax(s, axis=-1)
        a = p[:, :, 0] - lam * p[:, :, 1]
        return jnp.einsum('bhqk,bhke->bhqe', a.astype(vt.dtype), vt)

    o = lax.map(one_block, qb)
    o = o.transpose(1, 0, 3, 2, 4).reshape(B_, S_, DA_HEADS, DA_VDIM)
    o = rms_norm(o, subln_g) * (1.0 - lambda_init)
    return o.reshape(B_, S_, DA_WIDTH)


def setup_inputs(seed: int = 0) -> dict:
    key = jax.random.key(seed)
    ks = jax.random.split(key, 20)
    f32 = jnp.float32
    n = lambda k, shape, s: jax.random.normal(k, shape, f32) * s
    return {
        "x": n(ks[0], (BATCH, SEQ, D_MODEL), 1.0),
        "c": n(ks[1], (BATCH, D_MODEL), 1.0),
        "w_ada": n(ks[2], (DEPTH, D_MODEL, 6 * D_MODEL), 0.1 * D_MODEL ** -0.5),
        "b_ada": n(ks[3], (DEPTH, 6 * D_MODEL), 0.01),
        "w_in": n(ks[4], (DEPTH, D_MODEL, IN_COLS), D_MODEL ** -0.5),
        "rpb": n(ks[5], (DEPTH, NA_HEADS, 2 * NA_KR_MAX - 1, 2 * NA_KC - 1), 0.1),
        "lambda_q1": n(ks[6], (DEPTH, HEAD_DIM), 0.1),
        "lambda_k1": n(ks[7], (DEPTH, HEAD_DIM), 0.1),
        "lambda_q2": n(ks[8], (DEPTH, HEAD_DIM), 0.1),
        "lambda_k2": n(ks[9], (DEPTH, HEAD_DIM), 0.1),
        "subln_g": 1.0 + n(ks[10], (DEPTH, DA_VDIM), 0.02),
        "w_out": n(ks[11], (DEPTH, MIX_WIDTH, D_MODEL), BETA * MIX_WIDTH ** -0.5),
        "ln1_g": 1.0 + n(ks[12], (DEPTH, D_MODEL), 0.02),
        "ln1_b": n(ks[13], (DEPTH, D_MODEL), 0.02),
        "w_gate": n(ks[14], (DEPTH, D_MODEL, D_FF), D_MODEL ** -0.5),
        "w_up": n(ks[15], (DEPTH, D_MODEL, D_FF), D_MODEL ** -0.5),
        "w_down": n(ks[16], (DEPTH, D_FF, D_MODEL), BETA * D_FF ** -0.5),
        "ln2_g": 1.0 + n(ks[17], (DEPTH, D_MODEL), 0.02),
        "ln2_b": n(ks[18], (DEPTH, D_MODEL), 0.02),
    }


def reference(x, c, w_ada, b_ada, w_in, rpb, lambda_q1, lambda_k1, lambda_q2, lambda_k2,
              subln_g, w_out, ln1_g, ln1_b, w_gate, w_up, w_down, ln2_g, ln2_b):
    B_, S_, _ = x.shape
    c_act = jax.nn.silu(c)
    for l in range(DEPTH):
        lambda_init = lambda_init_for(l + 1)
        mod = c_act @ w_ada[l] + b_ada[l]
        shift1, scale1, gate1, shift2, scale2, gate2 = [m[:, None, :] for m in jnp.split(mod, 6, axis=-1)]

        h = layer_norm(x) * (1.0 + scale1) + shift1
        proj = h @ w_in[l]
        o1 = NA_WIDTH; o2 = 2 * NA_WIDTH; o3 = 3 * NA_WIDTH
        o4 = o3 + DA_QK; o5 = o4 + DA_QK
        na_q = proj[..., :o1].reshape(B_, S_, NA_HEADS, HEAD_DIM)
        na_k = proj[..., o1:o2].reshape(B_, S_, NA_HEADS, HEAD_DIM)
        na_v = proj[..., o2:o3].reshape(B_, S_, NA_HEADS, HEAD_DIM)
        da_q = rope(proj[..., o3:o4].reshape(B_, S_, DA_HEADS, 2, HEAD_DIM))
        da_k = rope(proj[..., o4:o5].reshape(B_, S_, DA_HEADS, 2, HEAD_DIM))
        da_v = proj[..., o5:].reshape(B_, S_, DA_HEADS, DA_VDIM)

        out_a = neighbourhood_attention(na_q, na_k, na_v, rpb[l])
        lam = (jnp.exp(jnp.sum(lambda_q1[l].astype(jnp.float32) * lambda_k1[l].astype(jnp.float32)))
               - jnp.exp(jnp.sum(lambda_q2[l].astype(jnp.float32) * lambda_k2[l].astype(jnp.float32)))
               + lambda_init)
        out_b = differential_attention(da_q, da_k, da_v, lam, subln_g[l], lambda_init)
        mix = jnp.concatenate([out_a, out_b], axis=-1) @ w_out[l]
        x = layer_norm(ALPHA * x + (1.0 + gate1) * mix, ln1_g[l], ln1_b[l])

        h = layer_norm(x) * (1.0 + scale2) + shift2
        ffn = (jax.nn.silu(h @ w_gate[l]) * (h @ w_up[l])) @ w_down[l]
        x = layer_norm(ALPHA * x + (1.0 + gate2) * ffn, ln2_g[l], ln2_b[l])
    return x
```

```python
import os
import math
import numpy as np
import concourse.bass as bass
import concourse.mybir as mybir
from concourse.bass_utils import run_bass_kernel_spmd

F32 = mybir.dt.float32
BF16 = mybir.dt.bfloat16
AF = mybir.ActivationFunctionType
ALU = mybir.AluOpType
AX = mybir.AxisListType

ALPHA = 2.0 ** 0.25
EPS = 1e-5
LAMBDA_INIT = 0.8 - 0.6 * math.exp(0.0)
NEG = -30000.0


class _Op:
    __slots__ = ("eng", "meth", "args", "kw", "deps", "needs_inc", "is_dma", "sem", "val")

    def __init__(self, eng, meth, args, kw, is_dma=False):
        self.eng = eng
        self.meth = meth
        self.args = args
        self.kw = kw
        self.deps = []
        self.needs_inc = False
        self.is_dma = is_dma
        self.sem = None
        self.val = None


class Prog:
    COMPUTE = ("pe", "act", "dve", "pool")

    def __init__(self, nc):
        self.nc = nc
        self.engs = {"pe": nc.tensor, "act": nc.scalar, "dve": nc.vector,
                     "pool": nc.gpsimd, "sp": nc.sync}
        self.q = {k: [] for k in self.engs}
        self.lastw = {}
        self.readers = {}
        self.last_on = {}
        self.esem = {k: nc.alloc_semaphore(name=f"es_{k}") for k in self.COMPUTE}
        self.dsem = {}
        self.dcount = {}
        self.out_dmas = []

    def _track(self, op, reads, writes):
        deps = op.deps
        for k in reads:
            w = self.lastw.get(k)
            if w is not None:
                deps.append(w)
        for k in writes:
            w = self.lastw.get(k)
            if w is not None:
                deps.append(w)
            deps.extend(self.readers.get(k, ()))
        for k in reads:
            self.readers.setdefault(k, []).append(op)
        for k in writes:
            self.lastw[k] = op
            self.readers[k] = []
        self.last_on[(op.eng, op.is_dma)] = op

    def op(self, eng, meth, *args, reads=(), writes=(), **kw):
        o = _Op(eng, meth, args, kw)
        self._track(o, reads, writes)
        self.q[eng].append(o)
        return o

    def dma(self, eng, out, in_, reads=(), writes=(), sem=None, is_output=False, **kw):
        if sem is None:
            sem = "d_" + str(writes[0] if writes else reads[0])
        if sem not in self.dsem:
            self.dsem[sem] = self.nc.alloc_semaphore(name="ds%d" % len(self.dsem))
            self.dcount[sem] = 0
        self.dcount[sem] += 16
        kw = dict(kw)
        kw["out"] = out
        kw["in_"] = in_
        o = _Op(eng, "dma_start", (), kw, is_dma=True)
        o.sem = self.dsem[sem]
        o.val = self.dcount[sem]
        self._track(o, reads, writes)
        self.q[eng].append(o)
        if is_output:
            self.out_dmas.append(o)
        return o

    def barrier(self):
        lasts = dict(self.last_on)
        for eng in self.engs:
            o = _Op(eng, None, (), {})
            for (e2, isd), l in lasts.items():
                if e2 != eng or isd:
                    o.deps.append(l)
            self.q[eng].append(o)
        return

    def _skip(self, d, o):
        return (not d.is_dma) and (not o.is_dma) and d.eng == o.eng and o.eng == "pe"

    def emit(self):
        for eng, ops in self.q.items():
            for o in ops:
                for d in o.deps:
                    if d.is_dma or self._skip(d, o):
                        continue
                    d.needs_inc = True
        for eng in self.COMPUTE:
            c = 0
            for o in self.q[eng]:
                if o.is_dma or o.meth is None:
                    continue
                if o.needs_inc:
                    c += 1
                    o.sem = self.esem[eng]
                    o.val = c
        nwaits = 0
        for eng, ops in self.q.items():
            e = self.engs[eng]
            known = {}
            for o in ops:
                need = {}
                for d in o.deps:
                    if d.sem is None or self._skip(d, o):
                        continue
                    sid = d.sem.num
                    if known.get(sid, 0) >= d.val:
                        continue
                    if sid not in need or need[sid][1] < d.val:
                        need[sid] = (d.sem, d.val)
                for sid, (s, v) in need.items():
                    e.wait_ge(s, v)
                    known[sid] = v
                    nwaits += 1
                if o.meth is None:
                    continue
                ins = getattr(e, o.meth)(*o.args, **o.kw)
                if o.is_dma:
                    ins.then_inc(o.sem, 16)
                elif o.needs_inc:
                    ins.then_inc(o.sem, 1)
        e = self.engs["sp"]
        done = {}
        for o in self.out_dmas:
            if done.get(o.sem.num, (None, 0))[1] < o.val:
                done[o.sem.num] = (o.sem, o.val)
        for sid, (s, v) in done.items():
            e.wait_ge(s, v)
        st = {k: len(v) for k, v in self.q.items()}
        st["waits"] = nwaits
        st["dsems"] = len(self.dsem)
        return st


class SBAlloc:
    def __init__(self, nc):
        self.nc = nc
        self.off = 16512
        self.top = 229344
        self.n = 0
        self.peak = 0

    def alloc(self, shape, dt):
        nb = 2 if dt == BF16 else 4
        for s in shape[1:]:
            nb *= s
        off = (self.off + 63) // 64 * 64
        self.off = off + nb
        self.peak = max(self.peak, self.off)
        assert self.off <= self.top, ("SBUF overflow", self.off, shape)
        self.n += 1
        self.last_off = off
        return self.nc.alloc_sbuf_tensor_at("t%d" % self.n, list(shape), dt, offset=off)

    def mark(self):
        return self.off

    def release(self, m):
        self.off = m


def build_nc(stop=None, dbg=None):
    nc = bass.Bass("TRN2", target_bir_lowering=False)

    def din(name, shape, dt=F32):
        return nc.dram_tensor(name, list(shape), dt, kind="ExternalInput").ap()

    x_all = din("x_all", [8192, 1024])
    x_na = din("x_na", [2560, 1024])
    cT = din("cT", [128, 8])
    w_ada = din("w_ada", [1024, 6144])
    b_adaT = din("b_adaT", [128, 48])
    w_in = din("w_in", [1024, 3072])
    w_out = din("w_out", [1024, 1024])
    w_gate = din("w_gate", [1024, 2816])
    w_up = din("w_up", [1024, 2816])
    w_down = din("w_down", [2816, 1024])
    nab = din("nab", [27, 128, 1024])
    cs_all = din("cs_all", [128, 64, 64])
    cs_q = din("cs_q", [128, 16, 64])
    lamv = din("lamv", [128, 256])
    sublnB = din("sublnB", [128, 128])
    lnp = din("lnp", [128, 4096])
    identf = din("identf", [128, 128])
    out = nc.dram_tensor("out", [2048, 1024], F32, kind="ExternalOutput").ap()

    dbg_outs = []

    def dbg_out(name, ap_sb, shape, dt, key):
        if dbg is None:
            return
        d = nc.dram_tensor("dbg_" + name, list(shape), dt, kind="ExternalOutput").ap()
        dbg_outs.append((name, d, ap_sb, key))
        dbg[name] = None

    p = Prog(nc)
    sb = SBAlloc(nc)
    ps_all = nc.alloc_psum_tensor("ps_all", [128, 4096], F32)
    pb = [ps_all[:, i * 512:(i + 1) * 512] for i in range(8)]
    pbb = [t.bitcast(BF16) for t in pb]

    ident_f = sb.alloc([128, 128], F32)
    ident_b = sb.alloc([128, 128], BF16)
    ones_f = sb.alloc([128, 128], F32)
    modT = sb.alloc([128, 48], F32)
    small = sb.alloc([128, 64], F32)
    gB = sb.alloc([128, 128], F32)
    out_b = nc.alloc_sbuf_tensor_at("out_b", [128, 16, 512], BF16, offset=229344 - 16384)
    nlam = small[:, 0:1]

    p.dma("sp", ident_f[:], identf, writes=["ident_f"])
    p.op("dve", "tensor_copy", out=ident_b[:], in_=ident_f[:], reads=["ident_f"], writes=["ident_b"])
    p.op("pool", "memset", ones_f[:], 1.0, writes=["ones_f"])

    c2 = sb.alloc([128, 8, 2], BF16)
    bada = sb.alloc([128, 48], F32)
    m0 = sb.mark()
    lam_sb = sb.alloc([128, 4, 64], F32)
    lam_t = sb.alloc([128, 2, 64], F32)
    p.dma("sp", lam_sb[:].rearrange("p a b -> p (a b)"), lamv, writes=["lam_sb"])
    p.dma("sp", gB[:], sublnB, writes=["gB0"])
    p.op("dve", "tensor_tensor", out=lam_t[:, 0, :], in0=lam_sb[:, 0, :], in1=lam_sb[:, 1, :], op=ALU.mult,
         reads=["lam_sb"], writes=["lam_t0"])
    p.op("dve", "tensor_tensor", out=lam_t[:, 1, :], in0=lam_sb[:, 2, :], in1=lam_sb[:, 3, :], op=ALU.mult,
         reads=["lam_sb"], writes=["lam_t1"])
    p.op("dve", "reduce_sum", out=small[:, 1:2], in_=lam_t[:, 0, :], axis=AX.X, reads=["lam_t0"], writes=["ls1"])
    p.op("dve", "reduce_sum", out=small[:, 2:3], in_=lam_t[:, 1, :], axis=AX.X, reads=["lam_t1"], writes=["ls2"])
    p.op("act", "activation", out=small[:, 3:5], in_=small[:, 1:3], func=AF.Exp, reads=["ls1", "ls2"], writes=["le"])
    p.op("dve", "tensor_tensor", out=small[:, 5:6], in0=small[:, 4:5], in1=small[:, 3:4], op=ALU.subtract,
         reads=["le"], writes=["ld"])
    p.op("dve", "tensor_scalar", out=small[:, 0:1], in0=small[:, 5:6], scalar1=-LAMBDA_INIT, scalar2=None, op0=ALU.add,
         reads=["ld"], writes=["nlam"])
    p.op("dve", "tensor_scalar", out=gB[:], in0=gB[:], scalar1=1.0 - LAMBDA_INIT, scalar2=None, op0=ALU.mult,
         reads=["gB0"], writes=["gB"])

    c_sb = sb.alloc([128, 8], F32)
    wst = [sb.alloc([128, 8, 512], BF16) for _ in range(4)]
    p.dma("sp", c_sb[:], cT, writes=["c_sb"])
    p.dma("sp", bada[:], b_adaT, writes=["bada"])
    p.op("act", "activation", out=c2[:, :, 0], in_=c_sb[:], func=AF.Silu, reads=["c_sb"], writes=["c2a"])
    p.op("act", "activation", out=c2[:, :, 1], in_=c_sb[:], func=AF.Silu, reads=["c_sb"], writes=["c2b"])

    def pm(c0, c1):
        return pb[7][:, 256 + 2 * c0:256 + 2 * c1]

    def a_group(g, slots, skey):
        s = g % len(slots)
        p.dma("pool", slots[s][:], w_ada[:, g * 512:(g + 1) * 512].rearrange("(kc p) c -> p kc c", p=128),
              writes=[(skey, s)])
        for j in range(4):
            col = g * 4 + j
            for kc in range(8):
                p.op("pe", "matmul", pm(col, col + 1), lhsT=slots[s][:, kc, j * 128:(j + 1) * 128],
                     rhs=c2[:, kc, :], start=(kc == 0), stop=(kc == 7), skip_group_check=True,
                     reads=[(skey, s), "c2a", "c2b"], writes=[("pmod", g), ("psum", 7)])

    for g in range(4):
        a_group(g, wst, "wst")
    p.op("dve", "tensor_tensor", out=modT[:, 0:16], in0=pm(0, 16)[:, 0:32:2], in1=bada[:, 0:16], op=ALU.add,
         reads=[("pmod", g) for g in range(4)] + ["bada"], writes=["modTa0"])
    p.op("dve", "tensor_scalar", out=modT[:, 8:16], in0=modT[:, 8:16], scalar1=1.0, scalar2=None, op0=ALU.add,
         reads=["modTa0"], writes=["modTa"])
    MODA = ["modTa"]
    MODB = ["modTb"]
    MOD = MODA
    dbg_out("modT", modT[:], [128, 48], F32, "modTb")
    dbg_out("nlam", small[:, 0:8], [128, 8], F32, "nlam")

    def finish():
        for (name, d, ap_sb, key) in dbg_outs:
            p.dma("sp", d, ap_sb, reads=[key], sem="dbg_" + name, is_output=True)
        st = p.emit()
        st["sb_peak"] = sb.peak
        print("PROG", st, flush=True)
        return nc

    p.barrier()
    sb.release(m0)

    def ln_tile(tag, s, x_src, xs, zt, stt, sc_col, sh_col, tr_bank, hT_dst, hkey, x_key=None, from_sb=None):
        if from_sb is None:
            p.dma("sp", xs[:], x_src, writes=[(tag, "xs", s)])
            xin = xs
            xk = (tag, "xs", s)
        else:
            xin = from_sb
            xk = x_key
        p.op("dve", "bn_stats", out=stt[:, 0:6], in_=xin[:, 0:512], reads=[xk], writes=[(tag, "st0", s)])
        p.op("dve", "bn_stats", out=stt[:, 6:12], in_=xin[:, 512:1024], reads=[xk], writes=[(tag, "st1", s)])
        p.op("dve", "bn_aggr", out=stt[:, 12:14], in_=stt[:, 0:12], reads=[(tag, "st0", s), (tag, "st1", s)],
             writes=[(tag, "mv", s)])
        p.op("act", "activation", out=stt[:, 14:15], in_=stt[:, 13:14], func=AF.Sqrt, bias=EPS,
             reads=[(tag, "mv", s)], writes=[(tag, "sd", s)])
        p.op("dve", "reciprocal", out=stt[:, 15:16], in_=stt[:, 14:15], reads=[(tag, "sd", s)], writes=[(tag, "rstd", s)])
        p.op("dve", "scalar_tensor_tensor", out=stt[:, 16:17], in0=stt[:, 12:13], scalar=-1.0, in1=stt[:, 15:16],
             op0=ALU.mult, op1=ALU.mult, reads=[(tag, "mv", s), (tag, "rstd", s)], writes=[(tag, "nb", s)])
        p.op("act", "activation", out=zt[:], in_=xin[:], func=AF.Identity, scale=stt[:, 15:16], bias=stt[:, 16:17],
             reads=[xk, (tag, "rstd", s), (tag, "nb", s)], writes=[(tag, "zt", s)])
        trk = ("psum", tr_bank)
        for fc in range(8):
            p.op("pe", "transpose", out=pbb[tr_bank][:, fc * 128:(fc + 1) * 128], in_=zt[:, fc * 128:(fc + 1) * 128],
                 identity=ident_b[:], reads=[(tag, "zt", s), "ident_b"], writes=[trk])
        for fc in range(8):
            src = pbb[tr_bank][:, fc * 128:(fc + 1) * 128]
            if fc % 2 == 0:
                p.op("dve", "tensor_scalar", out=hT_dst(fc), in0=src, scalar1=modT[:, sc_col + fc:sc_col + fc + 1],
                     scalar2=modT[:, sh_col + fc:sh_col + fc + 1], op0=ALU.mult, op1=ALU.add,
                     reads=[trk] + MOD, writes=[hkey])
            else:
                p.op("act", "activation", out=hT_dst(fc), in_=src, func=AF.Identity,
                     scale=modT[:, sc_col + fc:sc_col + fc + 1], bias=modT[:, sh_col + fc:sh_col + fc + 1],
                     reads=[trk] + MOD, writes=[hkey])

    def run_pipeline(T, stages, order=None, hook=None):
        n = len(stages)
        if order is None:
            order = list(range(n - 1, -1, -1))
        for step in range(T + n - 1):
            if hook is not None:
                hook(step)
            for k in order:
                t = step - k
                if 0 <= t < T:
                    stages[k](t)

    class LNPipe:
        def __init__(self, tag, nx, xsrc, sc_col, sh_col, tr_banks, hT_dst, hkey, xin=None, modkeys=None):
            self.mod = MODA if modkeys is None else modkeys
            self.tag, self.nx, self.xsrc, self.xin = tag, nx, xsrc, xin
            self.sc, self.sh, self.trb, self.hT_dst, self.hkey = sc_col, sh_col, tr_banks, hT_dst, hkey
            self.xs = [sb.alloc([128, 1024], F32) for _ in range(nx)] if xsrc is not None else None
            self.stt = [sb.alloc([128, 32], F32) for _ in range(4)]
            self.zt = [sb.alloc([128, 1024], BF16) for _ in range(2)]

        def x(self, t):
            if self.xsrc is not None:
                return self.xs[t % self.nx], (self.tag, "xs", t % self.nx)
            return self.xin(t)

        def s_load(self, t):
            if self.xsrc is not None:
                xt, xk = self.x(t)
                p.dma("sp", xt[:], self.xsrc(t), writes=[xk])

        def s_stats(self, t):
            xt, xk = self.x(t)
            st, tg, s = self.stt[t % 4], self.tag, t % 4
            p.op("dve", "bn_stats", out=st[:, 0:6], in_=xt[:, 0:512], reads=[xk], writes=[(tg, "st0", s)])
            p.op("dve", "bn_stats", out=st[:, 6:12], in_=xt[:, 512:1024], reads=[xk], writes=[(tg, "st1", s)])
            p.op("dve", "bn_aggr", out=st[:, 12:14], in_=st[:, 0:12], reads=[(tg, "st0", s), (tg, "st1", s)],
                 writes=[(tg, "mv", s)])

        def s_rstd(self, t):
            st, tg, s = self.stt[t % 4], self.tag, t % 4
            p.op("act", "activation", out=st[:, 14:15], in_=st[:, 13:14], func=AF.Ln, bias=EPS,
                 reads=[(tg, "mv", s)], writes=[(tg, "lnv", s)])
            p.op("act", "activation", out=st[:, 15:16], in_=st[:, 14:15], func=AF.Exp, scale=-0.5,
                 reads=[(tg, "lnv", s)], writes=[(tg, "rstd", s)])

        def s_norm(self, t):
            xt, xk = self.x(t)
            st, tg, s = self.stt[t % 4], self.tag, t % 4
            p.op("dve", "tensor_scalar", out=self.zt[t % 2][:], in0=xt[:], scalar1=st[:, 12:13], scalar2=st[:, 15:16],
                 op0=ALU.subtract, op1=ALU.mult, reads=[xk, (tg, "mv", s), (tg, "rstd", s)], writes=[(tg, "zt", t % 2)])

        def s_rn_act(self, t):
            xt, xk = self.x(t)
            st, tg, s = self.stt[t % 4], self.tag, t % 4
            p.op("act", "activation", out=st[:, 14:15], in_=st[:, 13:14], func=AF.Ln, bias=EPS,
                 reads=[(tg, "mv", s)], writes=[(tg, "lnv", s)])
            p.op("act", "activation", out=st[:, 15:16], in_=st[:, 14:15], func=AF.Exp, scale=-0.5,
                 reads=[(tg, "lnv", s)], writes=[(tg, "rstd", s)])
            p.op("act", "activation", out=st[:, 17:18], in_=st[:, 12:13], func=AF.Identity, scale=st[:, 15:16],
                 reads=[(tg, "mv", s), (tg, "rstd", s)], writes=[(tg, "mr", s)])
            p.op("act", "activation", out=st[:, 16:17], in_=st[:, 17:18], func=AF.Identity, scale=-1.0,
                 reads=[(tg, "mr", s)], writes=[(tg, "nb", s)])
            p.op("act", "activation", out=self.zt[t % 2][:], in_=xt[:], func=AF.Identity, scale=st[:, 15:16],
                 bias=st[:, 16:17], reads=[xk, (tg, "rstd", s), (tg, "nb", s)], writes=[(tg, "zt", t % 2)])

        def s_te_act(self, t):
            self.s_tr(t)
            bk = self.trb[t % 2]
            for fc in range(8):
                src = pbb[bk][:, fc * 128:(fc + 1) * 128]
                sc = modT[:, self.sc + fc:self.sc + fc + 1]
                sh = modT[:, self.sh + fc:self.sh + fc + 1]
                p.op("act", "activation", out=self.hT_dst(t, fc), in_=src, func=AF.Identity, scale=sc, bias=sh,
                     reads=[("psum", bk)] + self.mod, writes=[self.hkey(t)])

        def s_tr(self, t):
            bk = self.trb[t % 2]
            z = self.zt[t % 2]
            for fc in range(8):
                p.op("pe", "transpose", out=pbb[bk][:, fc * 128:(fc + 1) * 128], in_=z[:, fc * 128:(fc + 1) * 128],
                     identity=ident_b[:], reads=[(self.tag, "zt", t % 2), "ident_b"], writes=[("psum", bk)])

        def s_evac(self, t):
            bk = self.trb[t % 2]
            for fc in range(8):
                src = pbb[bk][:, fc * 128:(fc + 1) * 128]
                sc = modT[:, self.sc + fc:self.sc + fc + 1]
                sh = modT[:, self.sh + fc:self.sh + fc + 1]
                if fc % 3 == 0:
                    p.op("dve", "tensor_scalar", out=self.hT_dst(t, fc), in0=src, scalar1=sc, scalar2=sh, op0=ALU.mult,
                         op1=ALU.add, reads=[("psum", bk)] + self.mod, writes=[self.hkey(t)])
                else:
                    p.op("act", "activation", out=self.hT_dst(t, fc), in_=src, func=AF.Identity, scale=sc, bias=sh,
                         reads=[("psum", bk)] + self.mod, writes=[self.hkey(t)])

    KT = sb.alloc([128, 4, 8192], BF16)
    VX = sb.alloc([128, 64, 4, 130], BF16)
    QT = sb.alloc([128, 4, 2048], BF16)
    wst2 = [nc.alloc_sbuf_tensor_at("wst2_%d" % i, [128, 8, 512], BF16, offset=sb.last_off + i * 8192) for i in range(2)]
    m1 = sb.mark()
    Wkv = sb.alloc([128, 8, 1024], BF16)
    Wq = sb.alloc([128, 8, 512], BF16)
    hT = [sb.alloc([128, 8, 128], BF16) for _ in range(2)]
    kf = [sb.alloc([128, 8, 2, 32], F32) for _ in range(2)]
    tA = sb.alloc([128, 8, 32], F32)
    tB = sb.alloc([128, 8, 32], F32)
    krot = [sb.alloc([128, 8, 2, 32], BF16) for _ in range(2)]
    cst = [sb.alloc([128, 64], F32) for _ in range(4)]

    p.dma("pool", Wkv[:], w_in[:, 2048:3072].rearrange("(kc p) c -> p kc c", p=128), writes=["Wkv"])
    p.dma("pool", Wq[:], w_in[:, 1536:2048].rearrange("(kc p) c -> p kc c", p=128), writes=["Wq"])
    p.op("pool", "memset", VX[:].rearrange("p a b c -> p (a b) c")[:, :, 128:130], 1.0, writes=["VXinit"])
    NKV = 64
    NT = 80
    lnB = LNPipe("B", 4,
                 lambda t: x_all[t * 128:(t + 1) * 128, :] if t < NKV else x_na[(t - NKV + 2) * 128:(t - NKV + 3) * 128, :],
                 8, 0, (0, 1), lambda t, fc: hT[t % 2][:, fc, :], lambda t: ("hT", t % 2))

    def b_evac(t):
        lnB.s_evac(t)
        src = cs_all[:, t, :] if t < NKV else cs_q[:, t - NKV, :]
        p.dma("sp", cst[t % 4][:], src, writes=[("cst", t % 4)])

    def b_proj(t):
        s = t % 2
        W, wk = (Wkv, "Wkv") if t < NKV else (Wq, "Wq")
        for kc in range(8):
            p.op("pe", "matmul", pb[2 + s][:, :], lhsT=hT[s][:, kc, :], rhs=W[:, kc, 0:512],
                 start=(kc == 0), stop=(kc == 7), reads=[("hT", s), wk], writes=[("psum", 2 + s)])
        if t < NKV:
            for kc in range(8):
                p.op("pe", "matmul", pb[4 + s][:, :], lhsT=hT[s][:, kc, :], rhs=Wkv[:, kc, 512:1024],
                     start=(kc == 0), stop=(kc == 7), reads=[("hT", s), "Wkv"], writes=[("psum", 4 + s)])

    def b_kvevac(t):
        s = t % 2
        p.op("act", "activation", out=kf[s][:].rearrange("p a b c -> p (a b c)"), in_=pb[2 + s][:, :], func=AF.Identity,
             reads=[("psum", 2 + s)], writes=[("kf", s)])
        if t < NKV:
            p.op("act", "activation", out=VX[:, t, :, 0:128], in_=pb[4 + s][:, :].rearrange("p (h d) -> p h d", h=4),
                 func=AF.Identity, reads=[("psum", 4 + s), "VXinit"], writes=[("VX", t)])

    def b_rope(t):
        s = t % 2
        c4 = t % 4
        x1 = kf[s][:, :, 0, :]
        x2 = kf[s][:, :, 1, :]
        cosb = cst[c4][:, 0:32].unsqueeze(1).broadcast_to([128, 8, 32])
        sinb = cst[c4][:, 32:64].unsqueeze(1).broadcast_to([128, 8, 32])
        kk, ck = ("kf", s), ("cst", c4)
        p.op("pool", "tensor_tensor", out=tA[:], in0=x1, in1=cosb, op=ALU.mult, reads=[kk, ck], writes=["tA"])
        p.op("pool", "tensor_tensor", out=tB[:], in0=x2, in1=sinb, op=ALU.mult, reads=[kk, ck], writes=["tB"])
        p.op("pool", "tensor_tensor", out=krot[s][:, :, 0, :], in0=tA[:], in1=tB[:], op=ALU.subtract,
             reads=["tA", "tB"], writes=[("krot0", s)])
        p.op("pool", "tensor_tensor", out=tA[:], in0=x2, in1=cosb, op=ALU.mult, reads=[kk, ck], writes=["tA"])
        p.op("pool", "tensor_tensor", out=tB[:], in0=x1, in1=sinb, op=ALU.mult, reads=[kk, ck], writes=["tB"])
        p.op("pool", "tensor_tensor", out=krot[s][:, :, 1, :], in0=tA[:], in1=tB[:], op=ALU.add,
             reads=["tA", "tB"], writes=[("krot1", s)])

    def b_ktr(t):
        s = t % 2
        kr = krot[s][:].rearrange("p a b c -> p (a b c)")
        for h in range(4):
            p.op("pe", "transpose", out=pbb[6 + s][:, h * 128:(h + 1) * 128], in_=kr[:, h * 128:(h + 1) * 128],
                 identity=ident_b[:], reads=[("krot0", s), ("krot1", s), "ident_b"], writes=[("psum", 6 + s)])

    def b_ktevac(t):
        s = t % 2
        if t < NKV:
            dst, dkey = KT[:, :, t * 128:(t + 1) * 128], ("KT", t)
        else:
            dst, dkey = QT[:, :, (t - NKV) * 128:(t - NKV + 1) * 128], ("QT", t - NKV)
        p.op("dve", "tensor_copy", out=dst, in_=pbb[6 + s][:, 0:512].rearrange("p (h k) -> p h k", h=4),
             reads=[("psum", 6 + s)], writes=[dkey])

    run_pipeline(NT, [lnB.s_load, lnB.s_stats, lnB.s_rstd, lnB.s_norm, lnB.s_tr, b_evac, b_proj, b_kvevac, b_rope,
                      b_ktr, b_ktevac], hook=lambda step: a_group(4 + (step - 8) // 3, wst2, "wst2")
                 if (step >= 8 and (step - 8) % 3 == 0 and (step - 8) // 3 < 8) else None)
    p.op("dve", "tensor_tensor", out=modT[:, 16:48], in0=pm(16, 48)[:, 0:64:2], in1=bada[:, 16:48], op=ALU.add,
         reads=[("pmod", g) for g in range(4, 12)] + ["bada", ("psum", 7)], writes=["modTb0"])
    p.op("dve", "tensor_scalar", out=modT[:, 16:24], in0=modT[:, 16:24], scalar1=1.0, scalar2=None, op0=ALU.add,
         reads=["modTb0"], writes=["modTb1"])
    p.op("dve", "tensor_scalar", out=modT[:, 32:48], in0=modT[:, 32:48], scalar1=1.0, scalar2=None, op0=ALU.add,
         reads=["modTb0", "modTb1"], writes=["modTb"])
    KTK = [("KT", t) for t in range(64)]
    VXK = [("VX", t) for t in range(64)]
    QTK = [("QT", t) for t in range(16)]
    dbg_out("KT", KT[:, 0, 0:1024], [128, 1024], BF16, ("KT", 7))
    dbg_out("QT", QT[:, 1, 0:1024], [128, 1024], BF16, ("QT", 7))
    dbg_out("VX", VX[:, 3, :, :].rearrange("p a b -> p (a b)"), [128, 520], BF16, ("VX", 3))
    if stop == "C":
        p.barrier()
        p.dma("sp", out[0:128, 0:48], modT[:], reads=["modTb"], sem="o", is_output=True)
        return finish()
    p.barrier()
    sb.release(m1)
    sb.top = 229344 - 16384

    Pb = [sb.alloc([128, 2, 512], BF16) for _ in range(3)]
    Osb = [sb.alloc([128, 8, 130], F32) for _ in range(2)]
    dsm = [sb.alloc([128, 32], F32) for _ in range(2)]
    of = [sb.alloc([128, 128], F32) for _ in range(2)]
    of2a = [sb.alloc([128, 4, 128], F32) for _ in range(2)]
    sq = sb.alloc([128, 128], F32)

    WNA_OFF = 229344 - 16384 - 24576
    assert sb.off <= WNA_OFF, sb.off
    Wna = nc.alloc_sbuf_tensor_at("Wna", [128, 8, 1536], BF16, offset=WNA_OFF)
    p.dma("pool", Wna[:], w_in[:, 0:1536].rearrange("(kc p) c -> p kc c", p=128), writes=["Wna"])

    its = [(h, qb, kt) for h in range(4) for qb in range(4) for kt in range(64)]

    def acc_ap(a, lo, hi):
        return pb[4 + a // 3][:, (a % 3) * 130 + lo:(a % 3) * 130 + hi]

    def da_qk(n):
        h, qb, kt = its[n]
        sl = n % 2
        for mp in range(2):
            p.op("pe", "matmul", pb[sl * 2 + mp][:, :], lhsT=KT[mp * 64:(mp + 1) * 64, h, kt * 128:(kt + 1) * 128],
                 rhs=QT[mp * 64:(mp + 1) * 64, h, qb * 512:(qb + 1) * 512], start=True, stop=True,
                 reads=[("KT", kt)] + QTK[qb * 4:qb * 4 + 4], writes=[("psum", sl * 2 + mp)])

    def da_exp(n):
        sl = n % 2
        p3 = n % 3
        p.op("act", "activation", out=Pb[p3][:].rearrange("p a b -> p (a b)"), in_=ps_all[:, sl * 1024:(sl + 1) * 1024],
             func=AF.Exp, scale=0.125, reads=[("psum", sl * 2), ("psum", sl * 2 + 1)], writes=[("P", p3)])

    def da_pv(n):
        h, qb, kt = its[n]
        p3 = n % 3
        for a in range(8):
            mp, qt = a // 4, a % 4
            p.op("pe", "matmul", acc_ap(a, 0, 129), lhsT=Pb[p3][:, mp, qt * 128:(qt + 1) * 128],
                 rhs=VX[:, kt, h, 0:129], start=(kt == 0 and a % 3 == 0), stop=(kt == 63), skip_group_check=True,
                 reads=[("P", p3), ("VX", kt)], writes=[("acc", a // 3)])

    def da_finish(n):
        h, qb, kt = its[n]
        blk = h * 4 + qb
        s = blk % 2
        O = Osb[s]
        for bk, (a0, na) in enumerate([(0, 3), (3, 3), (6, 2)]):
            p.op("dve", "tensor_copy", out=O[:, a0:a0 + na, 0:129],
                 in_=pb[4 + bk][:, 0:na * 130].rearrange("p (a c) -> p a c", c=130)[:, :, 0:129],
                 reads=[("acc", bk)], writes=[("Osb", s, bk)])
        ok = [("Osb", s, 0), ("Osb", s, 1), ("Osb", s, 2)]
        dm = dsm[s]
        p.op("dve", "reciprocal", out=dm[:, 0:8], in_=O[:, :, 128], reads=ok, writes=[("rinv", s)])
        p.op("dve", "tensor_scalar", out=dm[:, 8:12], in0=dm[:, 4:8], scalar1=nlam, scalar2=None, op0=ALU.mult,
             reads=[("rinv", s), "nlam"], writes=[("rn2", s)])
        for qt in range(4):
            u = qt % 2
            p.op("dve", "tensor_scalar", out=of[u][:], in0=O[:, qt, 0:128], scalar1=dm[:, qt:qt + 1], scalar2=None,
                 op0=ALU.mult, reads=ok + [("rinv", s)], writes=[("of", u)])
            p.op("dve", "scalar_tensor_tensor", out=of2a[s][:, qt, :], in0=O[:, 4 + qt, 0:128], scalar=dm[:, 8 + qt:9 + qt],
                 in1=of[u][:], op0=ALU.mult, op1=ALU.add, reads=ok + [("rn2", s), ("of", u)], writes=[("of2", s, qt)])
            p.op("pool", "tensor_tensor", out=sq[:], in0=of2a[s][:, qt, :], in1=of2a[s][:, qt, :], op=ALU.mult,
                 reads=[("of2", s, qt)], writes=["sq"])
            p.op("dve", "reduce_sum", out=dm[:, 12 + qt:13 + qt], in_=sq[:], axis=AX.X, reads=["sq"],
                 writes=[("ss", s, qt)])
        p.op("dve", "tensor_scalar", out=dm[:, 16:20], in0=dm[:, 12:16], scalar1=1.0 / 128.0, scalar2=EPS,
             op0=ALU.mult, op1=ALU.add, reads=[("ss", s, qt) for qt in range(4)], writes=[("ms", s)])

    def da_finish_b(n):
        h, qb, kt = its[n]
        blk = h * 4 + qb
        s = blk % 2
        dm = dsm[s]
        p.op("act", "activation", out=dm[:, 20:24], in_=dm[:, 16:20], func=AF.Ln, reads=[("ms", s)], writes=[("lnms", s)])
        p.op("act", "activation", out=dm[:, 24:28], in_=dm[:, 20:24], func=AF.Exp, scale=-0.5,
             reads=[("lnms", s)], writes=[("rr", s)])
        for qt in range(4):
            tile = qb * 4 + qt
            p.op("dve", "scalar_tensor_tensor", out=out_b[:, tile, h * 128:(h + 1) * 128], in0=of2a[s][:, qt, :],
                 scalar=dm[:, 24 + qt:25 + qt], in1=gB[:], op0=ALU.mult, op1=ALU.mult,
                 reads=[("of2", s, qt), ("rr", s), "gB"], writes=[("out_b", tile, h)])

    N = len(its)
    pending = []
    for n in range(N + 2):
        if n >= 2:
            da_exp(n - 2)
        if n < N:
            da_qk(n)
        if n >= 2:
            da_pv(n - 2)
            if its[n - 2][2] == 63:
                da_finish(n - 2)
                pending.append((n + 14, n - 2))
        while pending and (pending[0][0] <= n or n == N + 1):
            da_finish_b(pending.pop(0)[1])
    OBK = [("out_b", t, h) for t in range(16) for h in range(4)]
    dbg_out("out_b", out_b[:, 0, :], [128, 512], BF16, ("out_b", 0, 3))
    if stop == "D":
        p.barrier()
        p.dma("sp", out[0:128, 0:48], modT[:], reads=["modTb"], sem="o", is_output=True)
        return finish()
    p.barrier()
    sb.off = m0

    out_a = sb.alloc([128, 16, 512], BF16)
    qTn = sb.alloc([128, 4, 2048], BF16)
    kTn = sb.alloc([128, 4, 2560], BF16)
    vxn = sb.alloc([128, 20, 8, 66], BF16)
    EBi = sb.alloc([128, 5, 1024], BF16)
    nst = [sb.alloc([128, 1024], F32) for _ in range(2)]
    m2 = sb.mark()
    ebcnt = [0]

    def eb_unit(u):
        s_ = ebcnt[0] % 2
        ebcnt[0] += 1
        if 11 <= u < 16:
            dst = EBi[:, u - 11, :]
        else:
            dst = EBs[:, u if u < 11 else u - 5, :]
        p.dma("sp", nst[s_][:], nab[u], writes=[("nst", s_)])
        p.op("act", "activation", out=dst, in_=nst[s_][:], func=AF.Exp, reads=[("nst", s_)], writes=[("EB", u)])

    def eb_ap(u, par):
        if 11 <= u < 16:
            return EBi[:, u - 11, par * 512:(par + 1) * 512]
        return EBs[:, u if u < 11 else u - 5, par * 512:(par + 1) * 512]

    for u in range(11, 16):
        eb_unit(u)
    sb.top = WNA_OFF
    hTn = sb.alloc([128, 8, 2560], BF16)
    p.op("pool", "memset", vxn[:].rearrange("p a b c -> p (a b) c")[:, :, 64:66], 1.0, writes=["vxinit"])
    lnE = LNPipe("E", 4, lambda j: x_na[j * 128:(j + 1) * 128, :], 8, 0, (0, 1),
                 lambda j, fc: hTn[:, fc, j * 128:(j + 1) * 128], lambda j: ("hTn", j))

    def e_vproj(j):
        bk = 2 + j % 2
        for kc in range(8):
            p.op("pe", "matmul", pb[bk][:, :], lhsT=hTn[:, kc, j * 128:(j + 1) * 128], rhs=Wna[:, kc, 1024:1536],
                 start=(kc == 0), stop=(kc == 7), reads=[("hTn", j), "Wna"], writes=[("psum", bk)])

    def e_vevac(j):
        bk = 2 + j % 2
        p.op("dve", "tensor_copy", out=vxn[:, j, :, 0:64], in_=pb[bk][:, :].rearrange("p (h d) -> p h d", h=8),
             reads=[("psum", bk), "vxinit"], writes=[("vxn", j)])

    run_pipeline(20, [lnE.s_load, lnE.s_stats, lnE.s_rstd, lnE.s_norm, lnE.s_tr, lnE.s_evac, e_vproj, e_vevac])
    HK = [("hTn", j) for j in range(20)]
    cnt = 0
    for pc in range(4):
        for blk in range(5):
            bk = 4 + cnt % 2
            for kc in range(8):
                p.op("pe", "matmul", pb[bk][:, :], lhsT=Wna[:, kc, 512 + pc * 128:512 + (pc + 1) * 128],
                     rhs=hTn[:, kc, blk * 512:(blk + 1) * 512], start=(kc == 0), stop=(kc == 7),
                     reads=HK[blk * 4:blk * 4 + 4] + ["Wna"], writes=[("psum", bk)])
            eng = "act" if cnt % 2 == 0 else "dve"
            if eng == "act":
                p.op("act", "activation", out=kTn[:, pc, blk * 512:(blk + 1) * 512], in_=pb[bk][:, :], func=AF.Identity,
                     reads=[("psum", bk)], writes=[("kTn", pc, blk)])
            else:
                p.op("dve", "tensor_copy", out=kTn[:, pc, blk * 512:(blk + 1) * 512], in_=pb[bk][:, :],
                     reads=[("psum", bk)], writes=[("kTn", pc, blk)])
            cnt += 1
        for qblk in range(4):
            bk = 4 + cnt % 2
            for kc in range(8):
                p.op("pe", "matmul", pb[bk][:, :], lhsT=Wna[:, kc, pc * 128:(pc + 1) * 128],
                     rhs=hTn[:, kc, 256 + qblk * 512:256 + (qblk + 1) * 512], start=(kc == 0), stop=(kc == 7),
                     reads=HK[2 + qblk * 4:2 + qblk * 4 + 4] + ["Wna"], writes=[("psum", bk)])
            eng = "act" if cnt % 2 == 0 else "dve"
            if eng == "act":
                p.op("act", "activation", out=qTn[:, pc, qblk * 512:(qblk + 1) * 512], in_=pb[bk][:, :], func=AF.Identity,
                     reads=[("psum", bk)], writes=[("qTn", pc, qblk)])
            else:
                p.op("dve", "tensor_copy", out=qTn[:, pc, qblk * 512:(qblk + 1) * 512], in_=pb[bk][:, :],
                     reads=[("psum", bk)], writes=[("qTn", pc, qblk)])
            cnt += 1
    dbg_out("kTn", kTn[:, 0, 0:1024], [128, 1024], BF16, ("kTn", 0, 1))
    dbg_out("qTn", qTn[:, 0, 0:1024], [128, 1024], BF16, ("qTn", 0, 1))
    p.barrier()
    sb.release(m2)
    sb.top = 229344 - 16384
    EBs = sb.alloc([128, 22, 1024], BF16)
    Pn = [[sb.alloc([128, 512], BF16) for _ in range(2)] for _ in range(2)]
    rinvn = [sb.alloc([128, 8], F32) for _ in range(2)]
    pairs = []
    order = list(range(2, 14)) + [0, 1, 14, 15]
    for oi, i in enumerate(order):
        if i == 0:
            win, ub = list(range(0, 6)), 0
        elif i == 1:
            win, ub = list(range(1, 6)), 6
        elif i <= 13:
            win, ub = list(range(i, i + 5)), 11
        elif i == 14:
            win, ub = list(range(14, 19)), 16
        else:
            win, ub = list(range(14, 20)), 21
        for jj, j in enumerate(win):
            pairs.append((i, j, ub + jj, jj == 0, jj == len(win) - 1, oi % 2))
    sp_units = list(range(0, 11)) + list(range(16, 27))

    def na_keys_k(pc, j):
        return [("kTn", pc, j // 4)]

    def na_keys_q(pc, i):
        return [("qTn", pc, i // 4)]

    def na_qk(n):
        i, j, u, first, last, osl = pairs[n]
        sl = n % 2
        for pc in range(4):
            for par in range(2):
                p.op("pe", "matmul", pb[sl * 2 + par][:, pc * 128:(pc + 1) * 128],
                     lhsT=kTn[par * 64:(par + 1) * 64, pc, j * 128:(j + 1) * 128],
                     rhs=qTn[par * 64:(par + 1) * 64, pc, i * 128:(i + 1) * 128], start=True, stop=True,
                     skip_group_check=True,
                     reads=na_keys_k(pc, j) + na_keys_q(pc, i), writes=[("psum", sl * 2 + par)])

    def na_exp(n):
        i, j, u, first, last, osl = pairs[n]
        sl = n % 2
        for par in range(2):
            p.op("act", "activation", out=Pn[sl][par][:], in_=pb[sl * 2 + par][:, :], func=AF.Exp, scale=0.125,
                 reads=[("psum", sl * 2 + par)], writes=[("Pn0", sl, par), ("Pn", sl, par)])
            p.op("dve", "tensor_tensor", out=Pn[sl][par][:], in0=Pn[sl][par][:], in1=eb_ap(u, par),
                 op=ALU.mult, reads=[("Pn0", sl, par), ("EB", u)], writes=[("Pn", sl, par)])

    def na_pv(n):
        i, j, u, first, last, osl = pairs[n]
        sl = n % 2
        ob = 4 + osl * 2
        for pc in range(4):
            for par in range(2):
                p.op("pe", "matmul", pb[ob + par][:, pc * 66:pc * 66 + 65], lhsT=Pn[sl][par][:, pc * 128:(pc + 1) * 128],
                     rhs=vxn[:, j, 2 * pc + par, 0:65], start=(first and pc == 0), stop=last, skip_group_check=True,
                     reads=[("Pn", sl, par), ("vxn", j)], writes=[("psum", ob + par)])
        if last:
            s = osl
            for par in range(2):
                p.op("dve", "reciprocal", out=rinvn[s][:, par * 4:par * 4 + 4],
                     in_=pb[ob + par][:, 0:264].rearrange("p (a c) -> p a c", c=66)[:, :, 64],
                     reads=[("psum", ob + par)], writes=[("rinvn", s, par)])
                for pc in range(4):
                    h = 2 * pc + par
                    p.op("dve", "tensor_scalar", out=out_a[:, i, h * 64:(h + 1) * 64],
                         in0=pb[ob + par][:, pc * 66:pc * 66 + 64], scalar1=rinvn[s][:, par * 4 + pc:par * 4 + pc + 1],
                         scalar2=None, op0=ALU.mult, reads=[("psum", ob + par), ("rinvn", s, par)],
                         writes=[("out_a", i, h)])

    NP = len(pairs)
    for n in range(NP + 1):
        if n % 2 == 0 and n // 2 < len(sp_units):
            eb_unit(sp_units[n // 2])
        if n < NP:
            na_qk(n)
        if n >= 1:
            na_exp(n - 1)
            na_pv(n - 1)
    OAK = [("out_a", t, h) for t in range(16) for h in range(8)]
    dbg_out("out_a", out_a[:, 0, :], [128, 512], BF16, ("out_a", 0, 7))
    dbg_out("out_a15", out_a[:, 15, :], [128, 512], BF16, ("out_a", 15, 7))
    if stop == "E":
        p.barrier()
        p.dma("sp", out[0:128, 0:48], modT[:], reads=["modTb"], sem="o", is_output=True)
        return finish()
    p.barrier()
    sb.off = m0
    out_a2 = sb.alloc([128, 16, 512], BF16)
    assert out_a2[:].offset == out_a[:].offset

    x1 = sb.alloc([128, 16, 1024], F32)
    h2T = sb.alloc([128, 8, 2048], BF16)
    mF = sb.mark()
    Wo = sb.alloc([128, 8, 1024], BF16)
    lnp1 = sb.alloc([128, 2, 1024], F32)
    G1 = sb.alloc([128, 1024], F32)
    dg = sb.alloc([128, 128], F32)
    fxs = [sb.alloc([128, 1024], F32) for _ in range(3)]
    yb = [sb.alloc([128, 1024], F32) for _ in range(4)]
    cTt = [sb.alloc([128, 8, 128], BF16) for _ in range(2)]
    stR = [sb.alloc([128, 32], F32) for _ in range(4)]
    p.dma("pool", Wo[:], w_out.rearrange("(kc p) c -> p kc c", p=128), writes=["Wo"])
    p.dma("sp", lnp1[:].rearrange("p a b -> p (a b)"), lnp[:, 0:2048], writes=["lnp1"])

    def gate_bcast(g, Gt, gkey):
        for fc in range(8):
            col = (16 if g == 0 else 40) + fc
            p.op("dve", "tensor_scalar", out=dg[:], in0=ident_f[:], scalar1=modT[:, col:col + 1], scalar2=None,
                 op0=ALU.mult, reads=["ident_f"] + MODB, writes=["dg"])
            bk = fc // 4
            p.op("pe", "matmul", pb[bk][:, (fc % 4) * 128:(fc % 4 + 1) * 128], lhsT=ones_f[:], rhs=dg[:], start=True,
                 stop=True, skip_group_check=True, reads=["dg", "ones_f"], writes=[("psum", bk)])
            if fc % 4 == 3:
                p.op("dve", "tensor_copy", out=Gt[:, bk * 512:(bk + 1) * 512], in_=pb[bk][:, :],
                     reads=[("psum", bk)], writes=[(gkey, bk)])
        return [(gkey, 0), (gkey, 1)]

    G1K = gate_bcast(0, G1, "G1")

    def resid_stats(tag, y, YB, st_, s, pm_banks, Gt, gk, xres, xres_keys):
        for half in range(2):
            p.op("dve", "tensor_tensor", out=y[:, half * 512:(half + 1) * 512], in0=pb[pm_banks[half]][:, :],
                 in1=Gt[:, half * 512:(half + 1) * 512], op=ALU.mult,
                 reads=[("psum", pm_banks[half])] + gk, writes=[YB])
        p.op("dve", "scalar_tensor_tensor", out=y[:], in0=xres, scalar=ALPHA, in1=y[:], op0=ALU.mult, op1=ALU.add,
             reads=xres_keys, writes=[YB])
        p.op("dve", "bn_stats", out=st_[:, 0:6], in_=y[:, 0:512], reads=[YB], writes=[(tag, "st0", s)])
        p.op("dve", "bn_stats", out=st_[:, 6:12], in_=y[:, 512:1024], reads=[YB], writes=[(tag, "st1", s)])
        p.op("dve", "bn_aggr", out=st_[:, 12:14], in_=st_[:, 0:12], reads=[(tag, "st0", s), (tag, "st1", s)],
             writes=[(tag, "mv", s)])

    def resid_rstd(tag, st_, s):
        p.op("act", "activation", out=st_[:, 14:15], in_=st_[:, 13:14], func=AF.Ln, bias=EPS,
             reads=[(tag, "mv", s)], writes=[(tag, "lnv", s)])
        p.op("act", "activation", out=st_[:, 15:16], in_=st_[:, 14:15], func=AF.Exp, scale=-0.5,
             reads=[(tag, "lnv", s)], writes=[(tag, "rstd", s)])

    def resid_norm(tag, y, YB, st_, s):
        p.op("dve", "scalar_tensor_tensor", out=st_[:, 16:17], in0=st_[:, 12:13], scalar=-1.0, in1=st_[:, 15:16],
             op0=ALU.mult, op1=ALU.mult, reads=[(tag, "mv", s), (tag, "rstd", s)], writes=[(tag, "nb", s)])
        p.op("act", "activation", out=y[:], in_=y[:], func=AF.Identity, scale=st_[:, 15:16], bias=st_[:, 16:17],
             reads=[(tag, "rstd", s), (tag, "nb", s)], writes=[YB])

    def resid_affine(y, YB, lnt, lk, dst, dkey):
        p.op("dve", "tensor_tensor", out=y[:], in0=y[:], in1=lnt[:, 0, :], op=ALU.mult, reads=[lk], writes=[YB])
        if dst is None:
            p.op("dve", "tensor_tensor", out=y[:], in0=y[:], in1=lnt[:, 1, :], op=ALU.add, reads=[lk], writes=[YB])
        else:
            p.op("dve", "tensor_tensor", out=dst, in0=y[:], in1=lnt[:, 1, :], op=ALU.add, reads=[YB, lk], writes=[dkey])

    def f_tr(i):
        bk = i % 2
        for fc in range(8):
            src = out_a[:, i, fc * 128:(fc + 1) * 128] if fc < 4 else out_b[:, i, (fc - 4) * 128:(fc - 3) * 128]
            rk = [("out_a", i, 2 * fc), ("out_a", i, 2 * fc + 1)] if fc < 4 else [("out_b", i, fc - 4)]
            p.op("pe", "transpose", out=pbb[bk][:, fc * 128:(fc + 1) * 128], in_=src, identity=ident_b[:],
                 reads=rk + ["ident_b"], writes=[("psum", bk)])

    def f_cevac(i):
        bk = i % 2
        p.dma("sp", fxs[i % 3][:], x_na[(i + 2) * 128:(i + 3) * 128, :], writes=[("fxs", i % 3)])
        p.op("act", "activation", out=cTt[bk][:].rearrange("p a b -> p (a b)"), in_=pbb[bk][:, :], func=AF.Identity,
             reads=[("psum", bk)], writes=[("cTt", bk)])

    def f_mix(i):
        s = i % 2
        for half in range(2):
            bk = 2 + 2 * s + half
            for kc in range(8):
                p.op("pe", "matmul", pb[bk][:, :], lhsT=cTt[s][:, kc, :], rhs=Wo[:, kc, half * 512:(half + 1) * 512],
                     start=(kc == 0), stop=(kc == 7), reads=[("cTt", s), "Wo"], writes=[("psum", bk)])

    def f_y(i):
        s = i % 2
        resid_stats("R", yb[i % 4], ("yb", i % 4), stR[i % 4], i % 4, [2 + 2 * s, 3 + 2 * s], G1, G1K,
                    fxs[i % 3][:], [("fxs", i % 3)])

    def f_rn(i):
        st_, s_, y, YB = stR[i % 4], i % 4, yb[i % 4], ("yb", i % 4)
        resid_rstd("R", st_, s_)
        p.op("act", "activation", out=st_[:, 17:18], in_=st_[:, 12:13], func=AF.Identity, scale=st_[:, 15:16],
             reads=[("R", "mv", s_), ("R", "rstd", s_)], writes=[("R", "mr", s_)])
        p.op("act", "activation", out=st_[:, 16:17], in_=st_[:, 17:18], func=AF.Identity, scale=-1.0,
             reads=[("R", "mr", s_)], writes=[("R", "nb", s_)])
        p.op("act", "activation", out=y[:], in_=y[:], func=AF.Identity, scale=st_[:, 15:16], bias=st_[:, 16:17],
             reads=[("R", "rstd", s_), ("R", "nb", s_)], writes=[YB])

    def f_trc(i):
        f_tr(i)
        f_cevac(i)

    def f_aff(i):
        resid_affine(yb[i % 4], ("yb", i % 4), lnp1, "lnp1", x1[:, i, :], ("x1", i))

    lnF = LNPipe("F", 0, None, 32, 24, (6, 7), lambda i, fc: h2T[:, fc, i * 128:(i + 1) * 128], lambda i: ("h2T", i),
                 xin=lambda i: (x1[:, i, :], ("x1", i)), modkeys=MODB)
    run_pipeline(16, [f_trc, f_mix, f_y, f_rn, f_aff, lnF.s_stats, lnF.s_rn_act, lnF.s_te_act],
                 order=[3, 6, 7, 0, 1, 2, 4, 5])
    p.barrier()

    sb.top = 229344
    sb.off = m0
    G2 = sb.alloc([128, 1024], F32)
    lnp2 = sb.alloc([128, 2, 1024], F32)
    dg = sb.alloc([128, 128], F32)
    assert sb.off <= x1[:].offset if False else True
    sb.off = mF
    Wd = sb.alloc([128, 22, 1024], BF16)
    actT = sb.alloc([128, 22, 512], BF16)
    wg = [sb.alloc([128, 8, 128], BF16) for _ in range(2)]
    wu = [sb.alloc([128, 8, 128], BF16) for _ in range(2)]
    sg = [sb.alloc([128, 512], F32) for _ in range(2)]
    yb = [sb.alloc([128, 1024], F32) for _ in range(3)]
    stR = [sb.alloc([128, 32], F32) for _ in range(3)]
    p.dma("sp", lnp2[:].rearrange("p a b -> p (a b)"), lnp[:, 2048:4096], writes=["lnp2"])
    G2K = gate_bcast(1, G2, "G2")
    WDK = [("Wd", 0), ("Wd", 1)]
    ycnt = 0
    for tb in range(4):
        H2K = [("h2T", tb * 4 + ii) for ii in range(4)]
        for ffc in range(22):
            s = ffc % 2
            p.dma("pool", wg[s][:], w_gate[:, ffc * 128:(ffc + 1) * 128].rearrange("(kc p) c -> p kc c", p=128),
                  writes=[("wg", s)])
            p.dma("pool", wu[s][:], w_up[:, ffc * 128:(ffc + 1) * 128].rearrange("(kc p) c -> p kc c", p=128),
                  writes=[("wu", s)])
            if tb == 0 and ffc == 1:
                for half in range(2):
                    p.dma("pool", Wd[:, half * 11:(half + 1) * 11, :],
                          w_down[half * 1408:(half + 1) * 1408, :].rearrange("(fc p) c -> p fc c", p=128),
                          writes=[("Wd", half)])
            for kc in range(8):
                p.op("pe", "matmul", pb[4 + s][:, :], lhsT=wg[s][:, kc, :], rhs=h2T[:, kc, tb * 512:(tb + 1) * 512],
                     start=(kc == 0), stop=(kc == 7), reads=[("wg", s)] + H2K, writes=[("psum", 4 + s)])
            for kc in range(8):
                p.op("pe", "matmul", pb[6 + s][:, :], lhsT=wu[s][:, kc, :], rhs=h2T[:, kc, tb * 512:(tb + 1) * 512],
                     start=(kc == 0), stop=(kc == 7), reads=[("wu", s)] + H2K, writes=[("psum", 6 + s)])
            p.op("act", "activation", out=sg[s][:], in_=pb[4 + s][:, :], func=AF.Silu, reads=[("psum", 4 + s)],
                 writes=[("sg", s)])
            p.op("dve", "tensor_tensor", out=actT[:, ffc, :], in0=sg[s][:], in1=pb[6 + s][:, :], op=ALU.mult,
                 reads=[("sg", s), ("psum", 6 + s)], writes=[("actT", ffc)])
        AK = [("actT", f) for f in range(22)]
        for ii in range(4):
            i = tb * 4 + ii
            s = ii % 2
            for half in range(2):
                for ffc in range(22):
                    p.op("pe", "matmul", pb[2 * s + half][:, :], lhsT=actT[:, ffc, ii * 128:(ii + 1) * 128],
                         rhs=Wd[:, ffc, half * 512:(half + 1) * 512], start=(ffc == 0), stop=(ffc == 21),
                         reads=AK + WDK, writes=[("psum", 2 * s + half)])
            ys = ycnt % 3
            ycnt += 1
            YB = ("yb2", ys)
            resid_stats("R2", yb[ys], YB, stR[ys], ys, [2 * s, 2 * s + 1], G2, G2K, x1[:, i, :], [("x1", i)])
            resid_rstd("R2", stR[ys], ys)
            resid_norm("R2", yb[ys], YB, stR[ys], ys)
            resid_affine(yb[ys], YB, lnp2, "lnp2", None, None)
            p.dma("sp", out[i * 128:(i + 1) * 128, :], yb[ys][:], reads=[YB], sem="out%d" % ys, is_output=True)
    return finish()


def _na_bias_units(rpb, qr):
    rpb = np.asarray(rpb, np.float32)
    qrw = np.arange(128) // 64
    qc = np.arange(128) % 64
    krw = np.arange(128) // 64
    kc = np.arange(128) % 64
    cs = np.clip(qc - 8, 0, 48)
    colvalid = (kc[:, None] >= cs[None, :]) & (kc[:, None] < cs[None, :] + 16)
    col_rel = kc[:, None] - qc[None, :] + 15
    col_rel_c = np.clip(col_rel, 0, 30)
    units = []

    def unit(r0, kr0):
        r = r0 + qrw
        kr = kr0 + krw
        rs = np.clip(r - 4, 0, 120)
        rowvalid = (kr[:, None] >= rs[None, :]) & (kr[:, None] < rs[None, :] + 8) & (kr[:, None] >= 0) & (kr[:, None] < 128)
        row_rel = kr[:, None] - r[None, :] + 7
        row_rel_c = np.clip(row_rel, 0, 14)
        valid = rowvalid & colvalid
        g = rpb[:, row_rel_c, col_rel_c]
        g = np.where(valid[None], g, np.float32(NEG))
        g = g.reshape(4, 2, 128, 128).transpose(2, 1, 0, 3)
        return g.reshape(128, 1024)

    base = 32 * qr
    tiles_row0 = lambda j: base - 4 + 2 * j
    for j in range(0, 6):
        units.append(unit(base + 0, tiles_row0(j)))
    for j in range(1, 6):
        units.append(unit(base + 2, tiles_row0(j)))
    for jj in range(5):
        units.append(unit(64, 60 + 2 * jj))
    for j in range(14, 19):
        units.append(unit(base + 28, tiles_row0(j)))
    for j in range(14, 20):
        units.append(unit(base + 30, tiles_row0(j)))
    return np.ascontiguousarray(np.stack(units, 0), dtype=np.float32)


_NC_CACHE = {}


def _host_inputs(x, c, w_ada, b_ada, w_in, rpb, lambda_q1, lambda_k1, lambda_q2, lambda_k2,
                 subln_g, w_out, ln1_g, ln1_b, w_gate, w_up, w_down, ln2_g, ln2_b):
    f = lambda a: np.ascontiguousarray(np.asarray(a, dtype=np.float32))
    x = f(x)
    c = f(c)
    half = 32
    freqs = (1.0 / (np.float32(10000.0) ** (np.arange(half, dtype=np.float32) / np.float32(half)))).astype(np.float32)
    pos = np.arange(8192, dtype=np.float32)
    ang = (pos[:, None] * freqs[None, :]).astype(np.float32)
    cs = np.concatenate([np.cos(ang), np.sin(ang)], axis=1).astype(np.float32)
    cs_all = np.ascontiguousarray(cs.reshape(64, 128, 64).transpose(1, 0, 2))
    lamv = np.concatenate([f(lambda_q1)[0], f(lambda_k1)[0], f(lambda_q2)[0], f(lambda_k2)[0]])[None, :]
    lamv = np.ascontiguousarray(np.broadcast_to(lamv, (128, 256)))
    sublnB = np.ascontiguousarray(np.broadcast_to(f(subln_g)[0][None, :], (128, 128)))
    lnp = np.concatenate([f(ln1_g)[0], f(ln1_b)[0], f(ln2_g)[0], f(ln2_b)[0]])[None, :]
    lnp = np.ascontiguousarray(np.broadcast_to(lnp, (128, 4096)))
    identf = np.eye(128, dtype=np.float32)
    b_adaT = np.ascontiguousarray(f(b_ada)[0].reshape(48, 128).T)
    shared = dict(w_ada=f(w_ada)[0], b_adaT=b_adaT, w_in=f(w_in)[0], w_out=f(w_out)[0], w_gate=f(w_gate)[0],
                  w_up=f(w_up)[0], w_down=f(w_down)[0], cs_all=cs_all, lamv=lamv, sublnB=sublnB, lnp=lnp,
                  identf=identf)
    nabs = [_na_bias_units(f(rpb)[0], qr) for qr in range(4)]
    in_maps = []
    for core in range(8):
        b, qr = core // 4, core % 4
        xp = np.zeros((8192 + 512, 1024), np.float32)
        xp[256:256 + 8192] = x[b]
        m = dict(shared)
        m["x_all"] = x[b]
        m["x_na"] = np.ascontiguousarray(xp[2048 * qr:2048 * qr + 2560])
        m["cT"] = np.ascontiguousarray(c[b].reshape(8, 128).T)
        m["nab"] = nabs[qr]
        m["cs_q"] = np.ascontiguousarray(cs_all[:, 16 * qr:16 * qr + 16, :])
        in_maps.append(m)
    return in_maps


def kernel(**inputs):
    in_maps = _host_inputs(**inputs)
    if "nc" not in _NC_CACHE:
        _NC_CACHE["nc"] = build_nc()
    nc = _NC_CACHE["nc"]
    res = run_bass_kernel_spmd(nc, in_maps, core_ids=list(range(8)))
    outp = np.empty((2, 8192, 1024), np.float32)
    for core in range(8):
        b, qr = core // 4, core % 4
        outp[b, 2048 * qr:2048 * (qr + 1)] = res.results[core]["out"]
    return outp
```

```python
import os
import math
import numpy as np
import concourse.bass as bass
import concourse.mybir as mybir
from concourse.bass_utils import run_bass_kernel_spmd

F32 = mybir.dt.float32
BF16 = mybir.dt.bfloat16
AF = mybir.ActivationFunctionType
ALU = mybir.AluOpType
AX = mybir.AxisListType

ALPHA = 2.0 ** 0.25
EPS = 1e-5
LAMBDA_INIT = 0.8 - 0.6 * math.exp(0.0)
NEG = -30000.0


class _Op:
    __slots__ = ("eng", "meth", "args", "kw", "deps", "needs_inc", "is_dma", "sem", "val")

    def __init__(self, eng, meth, args, kw, is_dma=False):
        self.eng = eng
        self.meth = meth
        self.args = args
        self.kw = kw
        self.deps = []
        self.needs_inc = False
        self.is_dma = is_dma
        self.sem = None
        self.val = None


class Prog:
    COMPUTE = ("pe", "act", "dve", "pool")

    def __init__(self, nc):
        self.nc = nc
        self.engs = {"pe": nc.tensor, "act": nc.scalar, "dve": nc.vector,
                     "pool": nc.gpsimd, "sp": nc.sync}
        self.q = {k: [] for k in self.engs}
        self.lastw = {}
        self.readers = {}
        self.last_on = {}
        self.esem = {k: nc.alloc_semaphore(name=f"es_{k}") for k in self.COMPUTE}
        self.dsem = {}
        self.dcount = {}
        self.out_dmas = []

    def _track(self, op, reads, writes):
        deps = op.deps
        for k in reads:
            w = self.lastw.get(k)
            if w is not None:
                deps.append(w)
        for k in writes:
            w = self.lastw.get(k)
            if w is not None:
                deps.append(w)
            deps.extend(self.readers.get(k, ()))
        for k in reads:
            self.readers.setdefault(k, []).append(op)
        for k in writes:
            self.lastw[k] = op
            self.readers[k] = []
        self.last_on[(op.eng, op.is_dma)] = op

    def op(self, eng, meth, *args, reads=(), writes=(), **kw):
        o = _Op(eng, meth, args, kw)
        self._track(o, reads, writes)
        self.q[eng].append(o)
        return o

    def dma(self, eng, out, in_, reads=(), writes=(), sem=None, is_output=False, **kw):
        if sem is None:
            sem = "d_" + str(writes[0] if writes else reads[0])
        if sem not in self.dsem:
            self.dsem[sem] = self.nc.alloc_semaphore(name="ds%d" % len(self.dsem))
            self.dcount[sem] = 0
        self.dcount[sem] += 16
        kw = dict(kw)
        kw["out"] = out
        kw["in_"] = in_
        o = _Op(eng, "dma_start", (), kw, is_dma=True)
        o.sem = self.dsem[sem]
        o.val = self.dcount[sem]
        self._track(o, reads, writes)
        self.q[eng].append(o)
        if is_output:
            self.out_dmas.append(o)
        return o

    def barrier(self):
        lasts = dict(self.last_on)
        for eng in self.engs:
            o = _Op(eng, None, (), {})
            for (e2, isd), l in lasts.items():
                if e2 != eng or isd:
                    o.deps.append(l)
            self.q[eng].append(o)
        return

    def _skip(self, d, o):
        return (not d.is_dma) and (not o.is_dma) and d.eng == o.eng and o.eng == "pe"

    def emit(self):
        for eng, ops in self.q.items():
            for o in ops:
                for d in o.deps:
                    if d.is_dma or self._skip(d, o):
                        continue
                    d.needs_inc = True
        for eng in self.COMPUTE:
            c = 0
            for o in self.q[eng]:
                if o.is_dma or o.meth is None:
                    continue
                if o.needs_inc:
                    c += 1
                    o.sem = self.esem[eng]
                    o.val = c
        nwaits = 0
        for eng, ops in self.q.items():
            e = self.engs[eng]
            known = {}
            for o in ops:
                need = {}
                for d in o.deps:
                    if d.sem is None or self._skip(d, o):
                        continue
                    sid = d.sem.num
                    if known.get(sid, 0) >= d.val:
                        continue
                    if sid not in need or need[sid][1] < d.val:
                        need[sid] = (d.sem, d.val)
                for sid, (s, v) in need.items():
                    e.wait_ge(s, v)
                    known[sid] = v
                    nwaits += 1
                if o.meth is None:
                    continue
                ins = getattr(e, o.meth)(*o.args, **o.kw)
                if o.is_dma:
                    ins.then_inc(o.sem, 16)
                elif o.needs_inc:
                    ins.then_inc(o.sem, 1)
        e = self.engs["sp"]
        done = {}
        for o in self.out_dmas:
            if done.get(o.sem.num, (None, 0))[1] < o.val:
                done[o.sem.num] = (o.sem, o.val)
        for sid, (s, v) in done.items():
            e.wait_ge(s, v)
        st = {k: len(v) for k, v in self.q.items()}
        st["waits"] = nwaits
        st["dsems"] = len(self.dsem)
        return st


class SBAlloc:
    def __init__(self, nc):
        self.nc = nc
        self.off = 16512
        self.top = 229344
        self.n = 0
        self.peak = 0

    def alloc(self, shape, dt):
        nb = 2 if dt == BF16 else 4
        for s in shape[1:]:
            nb *= s
        off = (self.off + 63) // 64 * 64
        self.off = off + nb
        self.peak = max(self.peak, self.off)
        assert self.off <= self.top, ("SBUF overflow", self.off, shape)
        self.n += 1
        return self.nc.alloc_sbuf_tensor_at("t%d" % self.n, list(shape), dt, offset=off)

    def mark(self):
        return self.off

    def release(self, m):
        self.off = m


def build_nc(stop=None, dbg=None):
    nc = bass.Bass("TRN2", target_bir_lowering=False)

    def din(name, shape, dt=F32):
        return nc.dram_tensor(name, list(shape), dt, kind="ExternalInput").ap()

    x_all = din("x_all", [8192, 1024])
    x_na = din("x_na", [2560, 1024])
    cT = din("cT", [128, 8])
    w_ada = din("w_ada", [1024, 6144])
    b_adaT = din("b_adaT", [128, 48])
    w_in = din("w_in", [1024, 3072])
    w_out = din("w_out", [1024, 1024])
    w_gate = din("w_gate", [1024, 2816])
    w_up = din("w_up", [1024, 2816])
    w_down = din("w_down", [2816, 1024])
    nab = din("nab", [27, 128, 1024])
    cs_all = din("cs_all", [128, 64, 64])
    cs_q = din("cs_q", [128, 16, 64])
    lamv = din("lamv", [128, 256])
    sublnB = din("sublnB", [128, 128])
    lnp = din("lnp", [128, 4096])
    identf = din("identf", [128, 128])
    out = nc.dram_tensor("out", [2048, 1024], F32, kind="ExternalOutput").ap()

    dbg_outs = []

    def dbg_out(name, ap_sb, shape, dt, key):
        if dbg is None:
            return
        d = nc.dram_tensor("dbg_" + name, list(shape), dt, kind="ExternalOutput").ap()
        dbg_outs.append((name, d, ap_sb, key))
        dbg[name] = None

    p = Prog(nc)
    sb = SBAlloc(nc)
    ps_all = nc.alloc_psum_tensor("ps_all", [128, 4096], F32)
    pb = [ps_all[:, i * 512:(i + 1) * 512] for i in range(8)]
    pbb = [t.bitcast(BF16) for t in pb]

    ident_f = sb.alloc([128, 128], F32)
    ident_b = sb.alloc([128, 128], BF16)
    ones_f = sb.alloc([128, 128], F32)
    modT = sb.alloc([128, 48], F32)
    small = sb.alloc([128, 64], F32)
    gB = sb.alloc([128, 128], F32)
    out_b = nc.alloc_sbuf_tensor_at("out_b", [128, 16, 512], BF16, offset=229344 - 16384)
    nlam = small[:, 0:1]

    p.dma("sp", ident_f[:], identf, writes=["ident_f"])
    p.op("dve", "tensor_copy", out=ident_b[:], in_=ident_f[:], reads=["ident_f"], writes=["ident_b"])
    p.op("pool", "memset", ones_f[:], 1.0, writes=["ones_f"])

    m0 = sb.mark()
    lam_sb = sb.alloc([128, 4, 64], F32)
    lam_t = sb.alloc([128, 2, 64], F32)
    p.dma("sp", lam_sb[:].rearrange("p a b -> p (a b)"), lamv, writes=["lam_sb"])
    p.dma("sp", gB[:], sublnB, writes=["gB0"])
    p.op("dve", "tensor_tensor", out=lam_t[:, 0, :], in0=lam_sb[:, 0, :], in1=lam_sb[:, 1, :], op=ALU.mult,
         reads=["lam_sb"], writes=["lam_t0"])
    p.op("dve", "tensor_tensor", out=lam_t[:, 1, :], in0=lam_sb[:, 2, :], in1=lam_sb[:, 3, :], op=ALU.mult,
         reads=["lam_sb"], writes=["lam_t1"])
    p.op("dve", "reduce_sum", out=small[:, 1:2], in_=lam_t[:, 0, :], axis=AX.X, reads=["lam_t0"], writes=["ls1"])
    p.op("dve", "reduce_sum", out=small[:, 2:3], in_=lam_t[:, 1, :], axis=AX.X, reads=["lam_t1"], writes=["ls2"])
    p.op("act", "activation", out=small[:, 3:5], in_=small[:, 1:3], func=AF.Exp, reads=["ls1", "ls2"], writes=["le"])
    p.op("dve", "tensor_tensor", out=small[:, 5:6], in0=small[:, 4:5], in1=small[:, 3:4], op=ALU.subtract,
         reads=["le"], writes=["ld"])
    p.op("dve", "tensor_scalar", out=small[:, 0:1], in0=small[:, 5:6], scalar1=-LAMBDA_INIT, scalar2=None, op0=ALU.add,
         reads=["ld"], writes=["nlam"])
    p.op("dve", "tensor_scalar", out=gB[:], in0=gB[:], scalar1=1.0 - LAMBDA_INIT, scalar2=None, op0=ALU.mult,
         reads=["gB0"], writes=["gB"])

    c_sb = sb.alloc([128, 8], F32)
    c2 = sb.alloc([128, 8, 2], BF16)
    bada = sb.alloc([128, 48], F32)
    wst = [sb.alloc([128, 8, 512], BF16) for _ in range(4)]
    p.dma("sp", c_sb[:], cT, writes=["c_sb"])
    p.dma("sp", bada[:], b_adaT, writes=["bada"])
    p.op("act", "activation", out=c2[:, :, 0], in_=c_sb[:], func=AF.Silu, reads=["c_sb"], writes=["c2a"])
    p.op("act", "activation", out=c2[:, :, 1], in_=c_sb[:], func=AF.Silu, reads=["c_sb"], writes=["c2b"])
    pmod = pb[7]
    for g in range(12):
        s = g % 4
        p.dma("pool", wst[s][:], w_ada[:, g * 512:(g + 1) * 512].rearrange("(kc p) c -> p kc c", p=128),
              writes=[("wst", s)])
        for j in range(4):
            col = g * 4 + j
            for kc in range(8):
                p.op("pe", "matmul", pmod[:, col * 2:col * 2 + 2], lhsT=wst[s][:, kc, j * 128:(j + 1) * 128],
                     rhs=c2[:, kc, :], start=(kc == 0), stop=(kc == 7), skip_group_check=True,
                     reads=[("wst", s), "c2a", "c2b"], writes=["pmod"])
    p.op("dve", "tensor_tensor", out=modT[:], in0=pmod[:, 0:96:2], in1=bada[:], op=ALU.add,
         reads=["pmod", "bada"], writes=["modT0"])
    p.op("dve", "tensor_scalar", out=modT[:, 8:24], in0=modT[:, 8:24], scalar1=1.0, scalar2=None, op0=ALU.add,
         reads=["modT0"], writes=["modT1"])
    p.op("dve", "tensor_scalar", out=modT[:, 32:48], in0=modT[:, 32:48], scalar1=1.0, scalar2=None, op0=ALU.add,
         reads=["modT0", "modT1"], writes=["modT"])
    MOD = ["modT"]
    dbg_out("modT", modT[:], [128, 48], F32, "modT")
    dbg_out("nlam", small[:, 0:8], [128, 8], F32, "nlam")

    def finish():
        for (name, d, ap_sb, key) in dbg_outs:
            p.dma("sp", d, ap_sb, reads=[key], sem="dbg_" + name, is_output=True)
        st = p.emit()
        st["sb_peak"] = sb.peak
        print("PROG", st, flush=True)
        return nc

    if stop == "A":
        p.dma("sp", out[0:128, 0:48], modT[:], reads=["modT"], sem="o", is_output=True)
        return finish()
    p.barrier()
    sb.release(m0)

    def ln_tile(tag, s, x_src, xs, zt, stt, sc_col, sh_col, tr_bank, hT_dst, hkey, x_key=None, from_sb=None):
        if from_sb is None:
            p.dma("sp", xs[:], x_src, writes=[(tag, "xs", s)])
            xin = xs
            xk = (tag, "xs", s)
        else:
            xin = from_sb
            xk = x_key
        p.op("dve", "bn_stats", out=stt[:, 0:6], in_=xin[:, 0:512], reads=[xk], writes=[(tag, "st0", s)])
        p.op("dve", "bn_stats", out=stt[:, 6:12], in_=xin[:, 512:1024], reads=[xk], writes=[(tag, "st1", s)])
        p.op("dve", "bn_aggr", out=stt[:, 12:14], in_=stt[:, 0:12], reads=[(tag, "st0", s), (tag, "st1", s)],
             writes=[(tag, "mv", s)])
        p.op("act", "activation", out=stt[:, 14:15], in_=stt[:, 13:14], func=AF.Sqrt, bias=EPS,
             reads=[(tag, "mv", s)], writes=[(tag, "sd", s)])
        p.op("dve", "reciprocal", out=stt[:, 15:16], in_=stt[:, 14:15], reads=[(tag, "sd", s)], writes=[(tag, "rstd", s)])
        p.op("dve", "scalar_tensor_tensor", out=stt[:, 16:17], in0=stt[:, 12:13], scalar=-1.0, in1=stt[:, 15:16],
             op0=ALU.mult, op1=ALU.mult, reads=[(tag, "mv", s), (tag, "rstd", s)], writes=[(tag, "nb", s)])
        p.op("act", "activation", out=zt[:], in_=xin[:], func=AF.Identity, scale=stt[:, 15:16], bias=stt[:, 16:17],
             reads=[xk, (tag, "rstd", s), (tag, "nb", s)], writes=[(tag, "zt", s)])
        trk = ("psum", tr_bank)
        for fc in range(8):
            p.op("pe", "transpose", out=pbb[tr_bank][:, fc * 128:(fc + 1) * 128], in_=zt[:, fc * 128:(fc + 1) * 128],
                 identity=ident_b[:], reads=[(tag, "zt", s), "ident_b"], writes=[trk])
        for fc in range(8):
            src = pbb[tr_bank][:, fc * 128:(fc + 1) * 128]
            if fc % 2 == 0:
                p.op("dve", "tensor_scalar", out=hT_dst(fc), in0=src, scalar1=modT[:, sc_col + fc:sc_col + fc + 1],
                     scalar2=modT[:, sh_col + fc:sh_col + fc + 1], op0=ALU.mult, op1=ALU.add,
                     reads=[trk] + MOD, writes=[hkey])
            else:
                p.op("act", "activation", out=hT_dst(fc), in_=src, func=AF.Identity,
                     scale=modT[:, sc_col + fc:sc_col + fc + 1], bias=modT[:, sh_col + fc:sh_col + fc + 1],
                     reads=[trk] + MOD, writes=[hkey])

    def run_pipeline(T, stages, order=None):
        n = len(stages)
        if order is None:
            order = list(range(n - 1, -1, -1))
        for step in range(T + n - 1):
            for k in order:
                t = step - k
                if 0 <= t < T:
                    stages[k](t)

    def hk8(base):
        return [(base, fc) for fc in range(8)]

    class LNPipe:
        def __init__(self, tag, nx, xsrc, sc_col, sh_col, tr_banks, hT_dst, hkey, xin=None):
            self.tag, self.nx, self.xsrc, self.xin = tag, nx, xsrc, xin
            self.sc, self.sh, self.trb, self.hT_dst, self.hkey = sc_col, sh_col, tr_banks, hT_dst, hkey
            self.xs = [sb.alloc([128, 1024], F32) for _ in range(nx)] if xsrc is not None else None
            self.stt = [sb.alloc([128, 32], F32) for _ in range(4)]
            self.zt = [sb.alloc([128, 1024], BF16) for _ in range(2)]

        def x(self, t):
            if self.xsrc is not None:
                return self.xs[t % self.nx], (self.tag, "xs", t % self.nx)
            return self.xin(t)

        def s_load(self, t):
            if self.xsrc is not None:
                xt, xk = self.x(t)
                p.dma("sp", xt[:], self.xsrc(t), writes=[xk])

        def s_stats(self, t):
            xt, xk = self.x(t)
            st, tg, s = self.stt[t % 4], self.tag, t % 4
            p.op("dve", "bn_stats", out=st[:, 0:6], in_=xt[:, 0:512], reads=[xk], writes=[(tg, "st0", s)])
            p.op("dve", "bn_stats", out=st[:, 6:12], in_=xt[:, 512:1024], reads=[xk], writes=[(tg, "st1", s)])
            p.op("dve", "bn_aggr", out=st[:, 12:14], in_=st[:, 0:12], reads=[(tg, "st0", s), (tg, "st1", s)],
                 writes=[(tg, "mv", s)])

        def s_rstd(self, t):
            st, tg, s = self.stt[t % 4], self.tag, t % 4
            p.op("act", "activation", out=st[:, 14:15], in_=st[:, 13:14], func=AF.Ln, bias=EPS,
                 reads=[(tg, "mv", s)], writes=[(tg, "lnv", s)])
            p.op("act", "activation", out=st[:, 15:16], in_=st[:, 14:15], func=AF.Exp, scale=-0.5,
                 reads=[(tg, "lnv", s)], writes=[(tg, "rstd", s)])

        def s_norm(self, t):
            xt, xk = self.x(t)
            st, tg, s = self.stt[t % 4], self.tag, t % 4
            p.op("dve", "tensor_scalar", out=self.zt[t % 2][:], in0=xt[:], scalar1=st[:, 12:13], scalar2=st[:, 15:16],
                 op0=ALU.subtract, op1=ALU.mult, reads=[xk, (tg, "mv", s), (tg, "rstd", s)], writes=[(tg, "zt", t % 2)])

        def s_rn_act(self, t):
            xt, xk = self.x(t)
            st, tg, s = self.stt[t % 4], self.tag, t % 4
            p.op("act", "activation", out=st[:, 14:15], in_=st[:, 13:14], func=AF.Ln, bias=EPS,
                 reads=[(tg, "mv", s)], writes=[(tg, "lnv", s)])
            p.op("act", "activation", out=st[:, 15:16], in_=st[:, 14:15], func=AF.Exp, scale=-0.5,
                 reads=[(tg, "lnv", s)], writes=[(tg, "rstd", s)])
            p.op("act", "activation", out=st[:, 17:18], in_=st[:, 12:13], func=AF.Identity, scale=st[:, 15:16],
                 reads=[(tg, "mv", s), (tg, "rstd", s)], writes=[(tg, "mr", s)])
            p.op("act", "activation", out=st[:, 16:17], in_=st[:, 17:18], func=AF.Identity, scale=-1.0,
                 reads=[(tg, "mr", s)], writes=[(tg, "nb", s)])
            p.op("act", "activation", out=self.zt[t % 2][:], in_=xt[:], func=AF.Identity, scale=st[:, 15:16],
                 bias=st[:, 16:17], reads=[xk, (tg, "rstd", s), (tg, "nb", s)], writes=[(tg, "zt", t % 2)])

        def s_te_act(self, t):
            self.s_tr(t)
            bk = self.trb[t % 2]
            for fc in range(8):
                src = pbb[bk][:, fc * 128:(fc + 1) * 128]
                sc = modT[:, self.sc + fc:self.sc + fc + 1]
                sh = modT[:, self.sh + fc:self.sh + fc + 1]
                p.op("act", "activation", out=self.hT_dst(t, fc), in_=src, func=AF.Identity, scale=sc, bias=sh,
                     reads=[("psum", bk)] + MOD, writes=[(self.hkey(t), fc)])

        def s_tr(self, t):
            bk = self.trb[t % 2]
            z = self.zt[t % 2]
            for fc in range(8):
                p.op("pe", "transpose", out=pbb[bk][:, fc * 128:(fc + 1) * 128], in_=z[:, fc * 128:(fc + 1) * 128],
                     identity=ident_b[:], reads=[(self.tag, "zt", t % 2), "ident_b"], writes=[("psum", bk)])

        def s_evac(self, t):
            bk = self.trb[t % 2]
            for fc in range(8):
                src = pbb[bk][:, fc * 128:(fc + 1) * 128]
                sc = modT[:, self.sc + fc:self.sc + fc + 1]
                sh = modT[:, self.sh + fc:self.sh + fc + 1]
                if t % 2 == 0:
                    p.op("dve", "tensor_scalar", out=self.hT_dst(t, fc), in0=src, scalar1=sc, scalar2=sh, op0=ALU.mult,
                         op1=ALU.add, reads=[("psum", bk)] + MOD, writes=[(self.hkey(t), fc)])
                else:
                    p.op("act", "activation", out=self.hT_dst(t, fc), in_=src, func=AF.Identity, scale=sc, bias=sh,
                         reads=[("psum", bk)] + MOD, writes=[(self.hkey(t), fc)])

    KT = sb.alloc([128, 4, 8192], BF16)
    VX = sb.alloc([128, 64, 4, 130], BF16)
    QT = sb.alloc([128, 4, 2048], BF16)
    m1 = sb.mark()
    Wkv = sb.alloc([128, 8, 1024], BF16)
    Wq = sb.alloc([128, 8, 512], BF16)
    hT = [sb.alloc([128, 8, 128], BF16) for _ in range(2)]
    kf = [sb.alloc([128, 8, 2, 32], F32) for _ in range(2)]
    tA = sb.alloc([128, 8, 32], F32)
    tB = sb.alloc([128, 8, 32], F32)
    krot = [sb.alloc([128, 8, 2, 32], BF16) for _ in range(2)]
    cst = [sb.alloc([128, 64], F32) for _ in range(4)]

    p.dma("pool", Wkv[:], w_in[:, 2048:3072].rearrange("(kc p) c -> p kc c", p=128), writes=["Wkv"])
    p.dma("pool", Wq[:], w_in[:, 1536:2048].rearrange("(kc p) c -> p kc c", p=128), writes=["Wq"])
    p.op("pool", "memset", VX[:].rearrange("p a b c -> p (a b) c")[:, :, 128:130], 1.0, writes=["VXinit"])
    NKV = 64
    NT = 80
    lnB = LNPipe("B", 4,
                 lambda t: x_all[t * 128:(t + 1) * 128, :] if t < NKV else x_na[(t - NKV + 2) * 128:(t - NKV + 3) * 128, :],
                 8, 0, (0, 1), lambda t, fc: hT[t % 2][:, fc, :], lambda t: ("hT", t % 2))

    def b_evac(t):
        lnB.s_evac(t)
        src = cs_all[:, t, :] if t < NKV else cs_q[:, t - NKV, :]
        p.dma("sp", cst[t % 4][:], src, writes=[("cst", t % 4)])

    def b_proj(t):
        s = t % 2
        W, wk = (Wkv, "Wkv") if t < NKV else (Wq, "Wq")
        for kc in range(8):
            p.op("pe", "matmul", pb[2 + s][:, :], lhsT=hT[s][:, kc, :], rhs=W[:, kc, 0:512],
                 start=(kc == 0), stop=(kc == 7), reads=hk8(("hT", s)) + [wk], writes=[("psum", 2 + s)])
        if t < NKV:
            for kc in range(8):
                p.op("pe", "matmul", pb[4 + s][:, :], lhsT=hT[s][:, kc, :], rhs=Wkv[:, kc, 512:1024],
                     start=(kc == 0), stop=(kc == 7), reads=hk8(("hT", s)) + ["Wkv"], writes=[("psum", 4 + s)])

    def b_kvevac(t):
        s = t % 2
        p.op("act", "activation", out=kf[s][:].rearrange("p a b c -> p (a b c)"), in_=pb[2 + s][:, :], func=AF.Identity,
             reads=[("psum", 2 + s)], writes=[("kf", s)])
        if t < NKV:
            p.op("act", "activation", out=VX[:, t, :, 0:128], in_=pb[4 + s][:, :].rearrange("p (h d) -> p h d", h=4),
                 func=AF.Identity, reads=[("psum", 4 + s), "VXinit"], writes=[("VX", t)])

    def b_rope(t):
        s = t % 2
        c4 = t % 4
        x1 = kf[s][:, :, 0, :]
        x2 = kf[s][:, :, 1, :]
        cosb = cst[c4][:, 0:32].unsqueeze(1).broadcast_to([128, 8, 32])
        sinb = cst[c4][:, 32:64].unsqueeze(1).broadcast_to([128, 8, 32])
        kk, ck = ("kf", s), ("cst", c4)
        p.op("pool", "tensor_tensor", out=tA[:], in0=x1, in1=cosb, op=ALU.mult, reads=[kk, ck], writes=["tA"])
        p.op("pool", "tensor_tensor", out=tB[:], in0=x2, in1=sinb, op=ALU.mult, reads=[kk, ck], writes=["tB"])
        p.op("pool", "tensor_tensor", out=krot[s][:, :, 0, :], in0=tA[:], in1=tB[:], op=ALU.subtract,
             reads=["tA", "tB"], writes=[("krot0", s)])
        p.op("pool", "tensor_tensor", out=tA[:], in0=x2, in1=cosb, op=ALU.mult, reads=[kk, ck], writes=["tA"])
        p.op("pool", "tensor_tensor", out=tB[:], in0=x1, in1=sinb, op=ALU.mult, reads=[kk, ck], writes=["tB"])
        p.op("pool", "tensor_tensor", out=krot[s][:, :, 1, :], in0=tA[:], in1=tB[:], op=ALU.add,
             reads=["tA", "tB"], writes=[("krot1", s)])

    def b_ktr(t):
        s = t % 2
        kr = krot[s][:].rearrange("p a b c -> p (a b c)")
        for h in range(4):
            p.op("pe", "transpose", out=pbb[6 + s][:, h * 128:(h + 1) * 128], in_=kr[:, h * 128:(h + 1) * 128],
                 identity=ident_b[:], reads=[("krot0", s), ("krot1", s), "ident_b"], writes=[("psum", 6 + s)])

    def b_ktevac(t):
        s = t % 2
        if t < NKV:
            dst, dkey = KT[:, :, t * 128:(t + 1) * 128], ("KT", t)
        else:
            dst, dkey = QT[:, :, (t - NKV) * 128:(t - NKV + 1) * 128], ("QT", t - NKV)
        p.op("dve", "tensor_copy", out=dst, in_=pbb[6 + s][:, 0:512].rearrange("p (h k) -> p h k", h=4),
             reads=[("psum", 6 + s)], writes=[dkey])

    run_pipeline(NT, [lnB.s_load, lnB.s_stats, lnB.s_rstd, lnB.s_norm, lnB.s_tr, b_evac, b_proj, b_kvevac, b_rope,
                      b_ktr, b_ktevac])
    KTK = [("KT", t) for t in range(64)]
    VXK = [("VX", t) for t in range(64)]
    QTK = [("QT", t) for t in range(16)]
    dbg_out("KT", KT[:, 0, 0:1024], [128, 1024], BF16, ("KT", 7))
    dbg_out("QT", QT[:, 1, 0:1024], [128, 1024], BF16, ("QT", 7))
    dbg_out("VX", VX[:, 3, :, :].rearrange("p a b -> p (a b)"), [128, 520], BF16, ("VX", 3))
    if stop == "C":
        p.barrier()
        p.dma("sp", out[0:128, 0:48], modT[:], reads=["modT"], sem="o", is_output=True)
        return finish()
    p.barrier()
    sb.release(m1)
    sb.top = 229344 - 16384

    Pb = [sb.alloc([128, 2, 512], BF16) for _ in range(3)]
    Osb = [sb.alloc([128, 8, 130], F32) for _ in range(2)]
    dsm = [sb.alloc([128, 32], F32) for _ in range(2)]
    of = [sb.alloc([128, 128], F32) for _ in range(2)]
    of2a = [sb.alloc([128, 4, 128], F32) for _ in range(2)]
    sq = sb.alloc([128, 128], F32)

    WNA_OFF = 229344 - 16384 - 24576
    assert sb.off <= WNA_OFF, sb.off
    Wna = nc.alloc_sbuf_tensor_at("Wna", [128, 8, 1536], BF16, offset=WNA_OFF)
    p.dma("pool", Wna[:], w_in[:, 0:1536].rearrange("(kc p) c -> p kc c", p=128), writes=["Wna"])

    its = [(h, qb, kt) for h in range(4) for qb in range(4) for kt in range(64)]

    def acc_ap(a, lo, hi):
        return pb[4 + a // 3][:, (a % 3) * 130 + lo:(a % 3) * 130 + hi]

    def da_qk(n):
        h, qb, kt = its[n]
        sl = n % 2
        for mp in range(2):
            p.op("pe", "matmul", pb[sl * 2 + mp][:, :], lhsT=KT[mp * 64:(mp + 1) * 64, h, kt * 128:(kt + 1) * 128],
                 rhs=QT[mp * 64:(mp + 1) * 64, h, qb * 512:(qb + 1) * 512], start=True, stop=True,
                 reads=[("KT", kt)] + QTK[qb * 4:qb * 4 + 4], writes=[("psum", sl * 2 + mp)])

    def da_exp(n):
        sl = n % 2
        p3 = n % 3
        p.op("act", "activation", out=Pb[p3][:].rearrange("p a b -> p (a b)"), in_=ps_all[:, sl * 1024:(sl + 1) * 1024],
             func=AF.Exp, scale=0.125, reads=[("psum", sl * 2), ("psum", sl * 2 + 1)], writes=[("P", p3)])

    def da_pv(n):
        h, qb, kt = its[n]
        p3 = n % 3
        for a in range(8):
            mp, qt = a // 4, a % 4
            p.op("pe", "matmul", acc_ap(a, 0, 129), lhsT=Pb[p3][:, mp, qt * 128:(qt + 1) * 128],
                 rhs=VX[:, kt, h, 0:129], start=(kt == 0 and a % 3 == 0), stop=(kt == 63), skip_group_check=True,
                 reads=[("P", p3), ("VX", kt)], writes=[("acc", a // 3)])

    def da_finish(n):
        h, qb, kt = its[n]
        blk = h * 4 + qb
        s = blk % 2
        O = Osb[s]
        for bk, (a0, na) in enumerate([(0, 3), (3, 3), (6, 2)]):
            p.op("dve", "tensor_copy", out=O[:, a0:a0 + na, 0:129],
                 in_=pb[4 + bk][:, 0:na * 130].rearrange("p (a c) -> p a c", c=130)[:, :, 0:129],
                 reads=[("acc", bk)], writes=[("Osb", s, bk)])
        ok = [("Osb", s, 0), ("Osb", s, 1), ("Osb", s, 2)]
        dm = dsm[s]
        p.op("dve", "reciprocal", out=dm[:, 0:8], in_=O[:, :, 128], reads=ok, writes=[("rinv", s)])
        p.op("dve", "tensor_scalar", out=dm[:, 8:12], in0=dm[:, 4:8], scalar1=nlam, scalar2=None, op0=ALU.mult,
             reads=[("rinv", s), "nlam"], writes=[("rn2", s)])
        for qt in range(4):
            u = qt % 2
            p.op("dve", "tensor_scalar", out=of[u][:], in0=O[:, qt, 0:128], scalar1=dm[:, qt:qt + 1], scalar2=None,
                 op0=ALU.mult, reads=ok + [("rinv", s)], writes=[("of", u)])
            p.op("dve", "scalar_tensor_tensor", out=of2a[s][:, qt, :], in0=O[:, 4 + qt, 0:128], scalar=dm[:, 8 + qt:9 + qt],
                 in1=of[u][:], op0=ALU.mult, op1=ALU.add, reads=ok + [("rn2", s), ("of", u)], writes=[("of2", s, qt)])
            p.op("pool", "tensor_tensor", out=sq[:], in0=of2a[s][:, qt, :], in1=of2a[s][:, qt, :], op=ALU.mult,
                 reads=[("of2", s, qt)], writes=["sq"])
            p.op("dve", "reduce_sum", out=dm[:, 12 + qt:13 + qt], in_=sq[:], axis=AX.X, reads=["sq"],
                 writes=[("ss", s, qt)])
        p.op("dve", "tensor_scalar", out=dm[:, 16:20], in0=dm[:, 12:16], scalar1=1.0 / 128.0, scalar2=EPS,
             op0=ALU.mult, op1=ALU.add, reads=[("ss", s, qt) for qt in range(4)], writes=[("ms", s)])

    def da_finish_b(n):
        h, qb, kt = its[n]
        blk = h * 4 + qb
        s = blk % 2
        dm = dsm[s]
        p.op("act", "activation", out=dm[:, 20:24], in_=dm[:, 16:20], func=AF.Ln, reads=[("ms", s)], writes=[("lnms", s)])
        p.op("act", "activation", out=dm[:, 24:28], in_=dm[:, 20:24], func=AF.Exp, scale=-0.5,
             reads=[("lnms", s)], writes=[("rr", s)])
        for qt in range(4):
            tile = qb * 4 + qt
            p.op("dve", "scalar_tensor_tensor", out=out_b[:, tile, h * 128:(h + 1) * 128], in0=of2a[s][:, qt, :],
                 scalar=dm[:, 24 + qt:25 + qt], in1=gB[:], op0=ALU.mult, op1=ALU.mult,
                 reads=[("of2", s, qt), ("rr", s), "gB"], writes=[("out_b", tile, h)])

    N = len(its)
    pending = []
    for n in range(N + 2):
        if n >= 2:
            da_exp(n - 2)
        if n < N:
            da_qk(n)
        if n >= 2:
            da_pv(n - 2)
            if its[n - 2][2] == 63:
                da_finish(n - 2)
                pending.append((n + 14, n - 2))
        while pending and (pending[0][0] <= n or n == N + 1):
            da_finish_b(pending.pop(0)[1])
    OBK = [("out_b", t, h) for t in range(16) for h in range(4)]
    dbg_out("out_b", out_b[:, 0, :], [128, 512], BF16, ("out_b", 0, 3))
    if stop == "D":
        p.barrier()
        p.dma("sp", out[0:128, 0:48], modT[:], reads=["modT"], sem="o", is_output=True)
        return finish()
    p.barrier()
    sb.off = m0

    out_a = sb.alloc([128, 16, 512], BF16)
    qTn = sb.alloc([128, 4, 2048], BF16)
    kTn = sb.alloc([128, 4, 2560], BF16)
    vxn = sb.alloc([128, 20, 8, 66], BF16)
    EBi = sb.alloc([128, 5, 1024], BF16)
    nst = [sb.alloc([128, 1024], F32) for _ in range(2)]
    m2 = sb.mark()
    ebcnt = [0]

    def eb_unit(u):
        s_ = ebcnt[0] % 2
        ebcnt[0] += 1
        if 11 <= u < 16:
            dst = EBi[:, u - 11, :]
        else:
            dst = EBs[:, u if u < 11 else u - 5, :]
        p.dma("sp", nst[s_][:], nab[u], writes=[("nst", s_)])
        p.op("act", "activation", out=dst, in_=nst[s_][:], func=AF.Exp, reads=[("nst", s_)], writes=[("EB", u)])

    def eb_ap(u, par):
        if 11 <= u < 16:
            return EBi[:, u - 11, par * 512:(par + 1) * 512]
        return EBs[:, u if u < 11 else u - 5, par * 512:(par + 1) * 512]

    for u in range(11, 16):
        eb_unit(u)
    sb.top = WNA_OFF
    hTn = sb.alloc([128, 8, 2560], BF16)
    p.op("pool", "memset", vxn[:].rearrange("p a b c -> p (a b) c")[:, :, 64:66], 1.0, writes=["vxinit"])
    lnE = LNPipe("E", 4, lambda j: x_na[j * 128:(j + 1) * 128, :], 8, 0, (0, 1),
                 lambda j, fc: hTn[:, fc, j * 128:(j + 1) * 128], lambda j: ("hTn", j))

    def e_vproj(j):
        bk = 2 + j % 2
        for kc in range(8):
            p.op("pe", "matmul", pb[bk][:, :], lhsT=hTn[:, kc, j * 128:(j + 1) * 128], rhs=Wna[:, kc, 1024:1536],
                 start=(kc == 0), stop=(kc == 7), reads=hk8(("hTn", j)) + ["Wna"], writes=[("psum", bk)])

    def e_vevac(j):
        bk = 2 + j % 2
        p.op("dve", "tensor_copy", out=vxn[:, j, :, 0:64], in_=pb[bk][:, :].rearrange("p (h d) -> p h d", h=8),
             reads=[("psum", bk), "vxinit"], writes=[("vxn", j)])

    run_pipeline(20, [lnE.s_load, lnE.s_stats, lnE.s_rstd, lnE.s_norm, lnE.s_tr, lnE.s_evac, e_vproj, e_vevac])
    HK = [hk8(("hTn", j)) for j in range(20)]
    cnt = 0
    for pc in range(4):
        for blk in range(5):
            bk = 4 + cnt % 2
            for kc in range(8):
                p.op("pe", "matmul", pb[bk][:, :], lhsT=Wna[:, kc, 512 + pc * 128:512 + (pc + 1) * 128],
                     rhs=hTn[:, kc, blk * 512:(blk + 1) * 512], start=(kc == 0), stop=(kc == 7),
                     reads=sum(HK[blk * 4:blk * 4 + 4], []) + ["Wna"], writes=[("psum", bk)])
            eng = "act" if cnt % 2 == 0 else "dve"
            if eng == "act":
                p.op("act", "activation", out=kTn[:, pc, blk * 512:(blk + 1) * 512], in_=pb[bk][:, :], func=AF.Identity,
                     reads=[("psum", bk)], writes=[("kTn", pc, blk)])
            else:
                p.op("dve", "tensor_copy", out=kTn[:, pc, blk * 512:(blk + 1) * 512], in_=pb[bk][:, :],
                     reads=[("psum", bk)], writes=[("kTn", pc, blk)])
            cnt += 1
        for qblk in range(4):
            bk = 4 + cnt % 2
            for kc in range(8):
                p.op("pe", "matmul", pb[bk][:, :], lhsT=Wna[:, kc, pc * 128:(pc + 1) * 128],
                     rhs=hTn[:, kc, 256 + qblk * 512:256 + (qblk + 1) * 512], start=(kc == 0), stop=(kc == 7),
                     reads=sum(HK[2 + qblk * 4:2 + qblk * 4 + 4], []) + ["Wna"], writes=[("psum", bk)])
            eng = "act" if cnt % 2 == 0 else "dve"
            if eng == "act":
                p.op("act", "activation", out=qTn[:, pc, qblk * 512:(qblk + 1) * 512], in_=pb[bk][:, :], func=AF.Identity,
                     reads=[("psum", bk)], writes=[("qTn", pc, qblk)])
            else:
                p.op("dve", "tensor_copy", out=qTn[:, pc, qblk * 512:(qblk + 1) * 512], in_=pb[bk][:, :],
                     reads=[("psum", bk)], writes=[("qTn", pc, qblk)])
            cnt += 1
    dbg_out("kTn", kTn[:, 0, 0:1024], [128, 1024], BF16, ("kTn", 0, 1))
    dbg_out("qTn", qTn[:, 0, 0:1024], [128, 1024], BF16, ("qTn", 0, 1))
    p.barrier()
    sb.release(m2)
    sb.top = 229344 - 16384
    EBs = sb.alloc([128, 22, 1024], BF16)
    Pn = [[sb.alloc([128, 512], BF16) for _ in range(2)] for _ in range(2)]
    rinvn = [sb.alloc([128, 8], F32) for _ in range(2)]
    pairs = []
    order = list(range(2, 14)) + [0, 1, 14, 15]
    for oi, i in enumerate(order):
        if i == 0:
            win, ub = list(range(0, 6)), 0
        elif i == 1:
            win, ub = list(range(1, 6)), 6
        elif i <= 13:
            win, ub = list(range(i, i + 5)), 11
        elif i == 14:
            win, ub = list(range(14, 19)), 16
        else:
            win, ub = list(range(14, 20)), 21
        for jj, j in enumerate(win):
            pairs.append((i, j, ub + jj, jj == 0, jj == len(win) - 1, oi % 2))
    sp_units = list(range(0, 11)) + list(range(16, 27))

    def na_keys_k(pc, j):
        return [("kTn", pc, j // 4)]

    def na_keys_q(pc, i):
        return [("qTn", pc, i // 4)]

    def na_qk(n):
        i, j, u, first, last, osl = pairs[n]
        sl = n % 2
        for pc in range(4):
            for par in range(2):
                p.op("pe", "matmul", pb[sl * 2 + par][:, pc * 128:(pc + 1) * 128],
                     lhsT=kTn[par * 64:(par + 1) * 64, pc, j * 128:(j + 1) * 128],
                     rhs=qTn[par * 64:(par + 1) * 64, pc, i * 128:(i + 1) * 128], start=True, stop=True,
                     skip_group_check=True,
                     reads=na_keys_k(pc, j) + na_keys_q(pc, i), writes=[("psum", sl * 2 + par)])

    def na_exp(n):
        i, j, u, first, last, osl = pairs[n]
        sl = n % 2
        for par in range(2):
            p.op("act", "activation", out=Pn[sl][par][:], in_=pb[sl * 2 + par][:, :], func=AF.Exp, scale=0.125,
                 reads=[("psum", sl * 2 + par)], writes=[("Pn0", sl, par), ("Pn", sl, par)])
            p.op("dve", "tensor_tensor", out=Pn[sl][par][:], in0=Pn[sl][par][:], in1=eb_ap(u, par),
                 op=ALU.mult, reads=[("Pn0", sl, par), ("EB", u)], writes=[("Pn", sl, par)])

    def na_pv(n):
        i, j, u, first, last, osl = pairs[n]
        sl = n % 2
        ob = 4 + osl * 2
        for pc in range(4):
            for par in range(2):
                p.op("pe", "matmul", pb[ob + par][:, pc * 66:pc * 66 + 65], lhsT=Pn[sl][par][:, pc * 128:(pc + 1) * 128],
                     rhs=vxn[:, j, 2 * pc + par, 0:65], start=(first and pc == 0), stop=last, skip_group_check=True,
                     reads=[("Pn", sl, par), ("vxn", j)], writes=[("psum", ob + par)])
        if last:
            s = osl
            for par in range(2):
                p.op("dve", "reciprocal", out=rinvn[s][:, par * 4:par * 4 + 4],
                     in_=pb[ob + par][:, 0:264].rearrange("p (a c) -> p a c", c=66)[:, :, 64],
                     reads=[("psum", ob + par)], writes=[("rinvn", s, par)])
                for pc in range(4):
                    h = 2 * pc + par
                    p.op("dve", "tensor_scalar", out=out_a[:, i, h * 64:(h + 1) * 64],
                         in0=pb[ob + par][:, pc * 66:pc * 66 + 64], scalar1=rinvn[s][:, par * 4 + pc:par * 4 + pc + 1],
                         scalar2=None, op0=ALU.mult, reads=[("psum", ob + par), ("rinvn", s, par)],
                         writes=[("out_a", i, h)])

    NP = len(pairs)
    for n in range(NP + 1):
        if n % 2 == 0 and n // 2 < len(sp_units):
            eb_unit(sp_units[n // 2])
        if n < NP:
            na_qk(n)
        if n >= 1:
            na_exp(n - 1)
            na_pv(n - 1)
    OAK = [("out_a", t, h) for t in range(16) for h in range(8)]
    dbg_out("out_a", out_a[:, 0, :], [128, 512], BF16, ("out_a", 0, 7))
    dbg_out("out_a15", out_a[:, 15, :], [128, 512], BF16, ("out_a", 15, 7))
    if stop == "E":
        p.barrier()
        p.dma("sp", out[0:128, 0:48], modT[:], reads=["modT"], sem="o", is_output=True)
        return finish()
    p.barrier()
    sb.off = m0
    out_a2 = sb.alloc([128, 16, 512], BF16)
    assert out_a2[:].offset == out_a[:].offset

    x1 = sb.alloc([128, 16, 1024], F32)
    h2T = sb.alloc([128, 8, 2048], BF16)
    mF = sb.mark()
    Wo = sb.alloc([128, 8, 1024], BF16)
    lnp1 = sb.alloc([128, 2, 1024], F32)
    G1 = sb.alloc([128, 1024], F32)
    dg = sb.alloc([128, 128], F32)
    fxs = [sb.alloc([128, 1024], F32) for _ in range(3)]
    yb = [sb.alloc([128, 1024], F32) for _ in range(4)]
    cTt = [sb.alloc([128, 8, 128], BF16) for _ in range(2)]
    stR = [sb.alloc([128, 32], F32) for _ in range(4)]
    p.dma("pool", Wo[:], w_out.rearrange("(kc p) c -> p kc c", p=128), writes=["Wo"])
    p.dma("sp", lnp1[:].rearrange("p a b -> p (a b)"), lnp[:, 0:2048], writes=["lnp1"])

    def gate_bcast(g, Gt, gkey):
        for fc in range(8):
            col = (16 if g == 0 else 40) + fc
            p.op("dve", "tensor_scalar", out=dg[:], in0=ident_f[:], scalar1=modT[:, col:col + 1], scalar2=None,
                 op0=ALU.mult, reads=["ident_f"] + MOD, writes=["dg"])
            bk = fc // 4
            p.op("pe", "matmul", pb[bk][:, (fc % 4) * 128:(fc % 4 + 1) * 128], lhsT=ones_f[:], rhs=dg[:], start=True,
                 stop=True, skip_group_check=True, reads=["dg", "ones_f"], writes=[("psum", bk)])
            if fc % 4 == 3:
                p.op("dve", "tensor_copy", out=Gt[:, bk * 512:(bk + 1) * 512], in_=pb[bk][:, :],
                     reads=[("psum", bk)], writes=[(gkey, bk)])
        return [(gkey, 0), (gkey, 1)]

    G1K = gate_bcast(0, G1, "G1")

    def resid_stats(tag, y, YB, st_, s, pm_banks, Gt, gk, xres, xres_keys):
        for half in range(2):
            p.op("dve", "tensor_tensor", out=y[:, half * 512:(half + 1) * 512], in0=pb[pm_banks[half]][:, :],
                 in1=Gt[:, half * 512:(half + 1) * 512], op=ALU.mult,
                 reads=[("psum", pm_banks[half])] + gk, writes=[YB])
        p.op("dve", "scalar_tensor_tensor", out=y[:], in0=xres, scalar=ALPHA, in1=y[:], op0=ALU.mult, op1=ALU.add,
             reads=xres_keys, writes=[YB])
        p.op("dve", "bn_stats", out=st_[:, 0:6], in_=y[:, 0:512], reads=[YB], writes=[(tag, "st0", s)])
        p.op("dve", "bn_stats", out=st_[:, 6:12], in_=y[:, 512:1024], reads=[YB], writes=[(tag, "st1", s)])
        p.op("dve", "bn_aggr", out=st_[:, 12:14], in_=st_[:, 0:12], reads=[(tag, "st0", s), (tag, "st1", s)],
             writes=[(tag, "mv", s)])

    def resid_rstd(tag, st_, s):
        p.op("act", "activation", out=st_[:, 14:15], in_=st_[:, 13:14], func=AF.Ln, bias=EPS,
             reads=[(tag, "mv", s)], writes=[(tag, "lnv", s)])
        p.op("act", "activation", out=st_[:, 15:16], in_=st_[:, 14:15], func=AF.Exp, scale=-0.5,
             reads=[(tag, "lnv", s)], writes=[(tag, "rstd", s)])

    def resid_norm(tag, y, YB, st_, s):
        p.op("dve", "scalar_tensor_tensor", out=st_[:, 16:17], in0=st_[:, 12:13], scalar=-1.0, in1=st_[:, 15:16],
             op0=ALU.mult, op1=ALU.mult, reads=[(tag, "mv", s), (tag, "rstd", s)], writes=[(tag, "nb", s)])
        p.op("act", "activation", out=y[:], in_=y[:], func=AF.Identity, scale=st_[:, 15:16], bias=st_[:, 16:17],
             reads=[(tag, "rstd", s), (tag, "nb", s)], writes=[YB])

    def resid_affine(y, YB, lnt, lk, dst, dkey):
        p.op("dve", "tensor_tensor", out=y[:], in0=y[:], in1=lnt[:, 0, :], op=ALU.mult, reads=[lk], writes=[YB])
        if dst is None:
            p.op("dve", "tensor_tensor", out=y[:], in0=y[:], in1=lnt[:, 1, :], op=ALU.add, reads=[lk], writes=[YB])
        else:
            p.op("dve", "tensor_tensor", out=dst, in0=y[:], in1=lnt[:, 1, :], op=ALU.add, reads=[YB, lk], writes=[dkey])

    def f_tr(i):
        bk = i % 2
        for fc in range(8):
            src = out_a[:, i, fc * 128:(fc + 1) * 128] if fc < 4 else out_b[:, i, (fc - 4) * 128:(fc - 3) * 128]
            rk = [("out_a", i, 2 * fc), ("out_a", i, 2 * fc + 1)] if fc < 4 else [("out_b", i, fc - 4)]
            p.op("pe", "transpose", out=pbb[bk][:, fc * 128:(fc + 1) * 128], in_=src, identity=ident_b[:],
                 reads=rk + ["ident_b"], writes=[("psum", bk)])

    def f_cevac(i):
        bk = i % 2
        p.dma("sp", fxs[i % 3][:], x_na[(i + 2) * 128:(i + 3) * 128, :], writes=[("fxs", i % 3)])
        p.op("act", "activation", out=cTt[bk][:].rearrange("p a b -> p (a b)"), in_=pbb[bk][:, :], func=AF.Identity,
             reads=[("psum", bk)], writes=[("cTt", bk)])

    def f_mix(i):
        s = i % 2
        for half in range(2):
            bk = 2 + 2 * s + half
            for kc in range(8):
                p.op("pe", "matmul", pb[bk][:, :], lhsT=cTt[s][:, kc, :], rhs=Wo[:, kc, half * 512:(half + 1) * 512],
                     start=(kc == 0), stop=(kc == 7), reads=[("cTt", s), "Wo"], writes=[("psum", bk)])

    def f_y(i):
        s = i % 2
        resid_stats("R", yb[i % 4], ("yb", i % 4), stR[i % 4], i % 4, [2 + 2 * s, 3 + 2 * s], G1, G1K,
                    fxs[i % 3][:], [("fxs", i % 3)])

    def f_rn(i):
        st_, s_, y, YB = stR[i % 4], i % 4, yb[i % 4], ("yb", i % 4)
        resid_rstd("R", st_, s_)
        p.op("act", "activation", out=st_[:, 17:18], in_=st_[:, 12:13], func=AF.Identity, scale=st_[:, 15:16],
             reads=[("R", "mv", s_), ("R", "rstd", s_)], writes=[("R", "mr", s_)])
        p.op("act", "activation", out=st_[:, 16:17], in_=st_[:, 17:18], func=AF.Identity, scale=-1.0,
             reads=[("R", "mr", s_)], writes=[("R", "nb", s_)])
        p.op("act", "activation", out=y[:], in_=y[:], func=AF.Identity, scale=st_[:, 15:16], bias=st_[:, 16:17],
             reads=[("R", "rstd", s_), ("R", "nb", s_)], writes=[YB])

    def f_trc(i):
        f_tr(i)
        f_cevac(i)

    def f_aff(i):
        resid_affine(yb[i % 4], ("yb", i % 4), lnp1, "lnp1", x1[:, i, :], ("x1", i))

    lnF = LNPipe("F", 0, None, 32, 24, (6, 7), lambda i, fc: h2T[:, fc, i * 128:(i + 1) * 128], lambda i: ("h2T", i),
                 xin=lambda i: (x1[:, i, :], ("x1", i)))
    run_pipeline(16, [f_trc, f_mix, f_y, f_rn, f_aff, lnF.s_stats, lnF.s_rn_act, lnF.s_te_act],
                 order=[3, 6, 7, 0, 1, 2, 4, 5])
    p.barrier()

    sb.top = 229344
    sb.off = m0
    G2 = sb.alloc([128, 1024], F32)
    lnp2 = sb.alloc([128, 2, 1024], F32)
    dg = sb.alloc([128, 128], F32)
    assert sb.off <= x1[:].offset if False else True
    sb.off = mF
    Wd = sb.alloc([128, 22, 1024], BF16)
    actT = sb.alloc([128, 22, 512], BF16)
    wg = [sb.alloc([128, 8, 128], BF16) for _ in range(2)]
    wu = [sb.alloc([128, 8, 128], BF16) for _ in range(2)]
    sg = [sb.alloc([128, 512], F32) for _ in range(2)]
    yb = [sb.alloc([128, 1024], F32) for _ in range(3)]
    stR = [sb.alloc([128, 32], F32) for _ in range(3)]
    p.dma("sp", lnp2[:].rearrange("p a b -> p (a b)"), lnp[:, 2048:4096], writes=["lnp2"])
    G2K = gate_bcast(1, G2, "G2")
    WDK = [("Wd", 0), ("Wd", 1)]
    ycnt = 0
    for tb in range(4):
        H2K = sum([hk8(("h2T", tb * 4 + ii)) for ii in range(4)], [])
        for ffc in range(22):
            s = ffc % 2
            p.dma("pool", wg[s][:], w_gate[:, ffc * 128:(ffc + 1) * 128].rearrange("(kc p) c -> p kc c", p=128),
                  writes=[("wg", s)])
            p.dma("pool", wu[s][:], w_up[:, ffc * 128:(ffc + 1) * 128].rearrange("(kc p) c -> p kc c", p=128),
                  writes=[("wu", s)])
            if tb == 0 and ffc == 1:
                for half in range(2):
                    p.dma("pool", Wd[:, half * 11:(half + 1) * 11, :],
                          w_down[half * 1408:(half + 1) * 1408, :].rearrange("(fc p) c -> p fc c", p=128),
                          writes=[("Wd", half)])
            for kc in range(8):
                p.op("pe", "matmul", pb[4 + s][:, :], lhsT=wg[s][:, kc, :], rhs=h2T[:, kc, tb * 512:(tb + 1) * 512],
                     start=(kc == 0), stop=(kc == 7), reads=[("wg", s)] + H2K, writes=[("psum", 4 + s)])
            for kc in range(8):
                p.op("pe", "matmul", pb[6 + s][:, :], lhsT=wu[s][:, kc, :], rhs=h2T[:, kc, tb * 512:(tb + 1) * 512],
                     start=(kc == 0), stop=(kc == 7), reads=[("wu", s)] + H2K, writes=[("psum", 6 + s)])
            p.op("act", "activation", out=sg[s][:], in_=pb[4 + s][:, :], func=AF.Silu, reads=[("psum", 4 + s)],
                 writes=[("sg", s)])
            p.op("dve", "tensor_tensor", out=actT[:, ffc, :], in0=sg[s][:], in1=pb[6 + s][:, :], op=ALU.mult,
                 reads=[("sg", s), ("psum", 6 + s)], writes=[("actT", ffc)])
        AK = [("actT", f) for f in range(22)]
        for ii in range(4):
            i = tb * 4 + ii
            s = ii % 2
            for half in range(2):
                for ffc in range(22):
                    p.op("pe", "matmul", pb[2 * s + half][:, :], lhsT=actT[:, ffc, ii * 128:(ii + 1) * 128],
                         rhs=Wd[:, ffc, half * 512:(half + 1) * 512], start=(ffc == 0), stop=(ffc == 21),
                         reads=AK + WDK, writes=[("psum", 2 * s + half)])
            ys = ycnt % 3
            ycnt += 1
            YB = ("yb2", ys)
            resid_stats("R2", yb[ys], YB, stR[ys], ys, [2 * s, 2 * s + 1], G2, G2K, x1[:, i, :], [("x1", i)])
            resid_rstd("R2", stR[ys], ys)
            resid_norm("R2", yb[ys], YB, stR[ys], ys)
            resid_affine(yb[ys], YB, lnp2, "lnp2", None, None)
            p.dma("sp", out[i * 128:(i + 1) * 128, :], yb[ys][:], reads=[YB], sem="out%d" % ys, is_output=True)
    return finish()


def _na_bias_units(rpb, qr):
    rpb = np.asarray(rpb, np.float32)
    qrw = np.arange(128) // 64
    qc = np.arange(128) % 64
    krw = np.arange(128) // 64
    kc = np.arange(128) % 64
    cs = np.clip(qc - 8, 0, 48)
    colvalid = (kc[:, None] >= cs[None, :]) & (kc[:, None] < cs[None, :] + 16)
    col_rel = kc[:, None] - qc[None, :] + 15
    col_rel_c = np.clip(col_rel, 0, 30)
    units = []

    def unit(r0, kr0):
        r = r0 + qrw
        kr = kr0 + krw
        rs = np.clip(r - 4, 0, 120)
        rowvalid = (kr[:, None] >= rs[None, :]) & (kr[:, None] < rs[None, :] + 8) & (kr[:, None] >= 0) & (kr[:, None] < 128)
        row_rel = kr[:, None] - r[None, :] + 7
        row_rel_c = np.clip(row_rel, 0, 14)
        valid = rowvalid & colvalid
        g = rpb[:, row_rel_c, col_rel_c]
        g = np.where(valid[None], g, np.float32(NEG))
        g = g.reshape(4, 2, 128, 128).transpose(2, 1, 0, 3)
        return g.reshape(128, 1024)

    base = 32 * qr
    tiles_row0 = lambda j: base - 4 + 2 * j
    for j in range(0, 6):
        units.append(unit(base + 0, tiles_row0(j)))
    for j in range(1, 6):
        units.append(unit(base + 2, tiles_row0(j)))
    for jj in range(5):
        units.append(unit(64, 60 + 2 * jj))
    for j in range(14, 19):
        units.append(unit(base + 28, tiles_row0(j)))
    for j in range(14, 20):
        units.append(unit(base + 30, tiles_row0(j)))
    return np.ascontiguousarray(np.stack(units, 0), dtype=np.float32)


_NC_CACHE = {}


def _host_inputs(x, c, w_ada, b_ada, w_in, rpb, lambda_q1, lambda_k1, lambda_q2, lambda_k2,
                 subln_g, w_out, ln1_g, ln1_b, w_gate, w_up, w_down, ln2_g, ln2_b):
    f = lambda a: np.ascontiguousarray(np.asarray(a, dtype=np.float32))
    x = f(x)
    c = f(c)
    half = 32
    freqs = (1.0 / (np.float32(10000.0) ** (np.arange(half, dtype=np.float32) / np.float32(half)))).astype(np.float32)
    pos = np.arange(8192, dtype=np.float32)
    ang = (pos[:, None] * freqs[None, :]).astype(np.float32)
    cs = np.concatenate([np.cos(ang), np.sin(ang)], axis=1).astype(np.float32)
    cs_all = np.ascontiguousarray(cs.reshape(64, 128, 64).transpose(1, 0, 2))
    lamv = np.concatenate([f(lambda_q1)[0], f(lambda_k1)[0], f(lambda_q2)[0], f(lambda_k2)[0]])[None, :]
    lamv = np.ascontiguousarray(np.broadcast_to(lamv, (128, 256)))
    sublnB = np.ascontiguousarray(np.broadcast_to(f(subln_g)[0][None, :], (128, 128)))
    lnp = np.concatenate([f(ln1_g)[0], f(ln1_b)[0], f(ln2_g)[0], f(ln2_b)[0]])[None, :]
    lnp = np.ascontiguousarray(np.broadcast_to(lnp, (128, 4096)))
    identf = np.eye(128, dtype=np.float32)
    b_adaT = np.ascontiguousarray(f(b_ada)[0].reshape(48, 128).T)
    shared = dict(w_ada=f(w_ada)[0], b_adaT=b_adaT, w_in=f(w_in)[0], w_out=f(w_out)[0], w_gate=f(w_gate)[0],
                  w_up=f(w_up)[0], w_down=f(w_down)[0], cs_all=cs_all, lamv=lamv, sublnB=sublnB, lnp=lnp,
                  identf=identf)
    nabs = [_na_bias_units(f(rpb)[0], qr) for qr in range(4)]
    in_maps = []
    for core in range(8):
        b, qr = core // 4, core % 4
        xp = np.zeros((8192 + 512, 1024), np.float32)
        xp[256:256 + 8192] = x[b]
        m = dict(shared)
        m["x_all"] = x[b]
        m["x_na"] = np.ascontiguousarray(xp[2048 * qr:2048 * qr + 2560])
        m["cT"] = np.ascontiguousarray(c[b].reshape(8, 128).T)
        m["nab"] = nabs[qr]
        m["cs_q"] = np.ascontiguousarray(cs_all[:, 16 * qr:16 * qr + 16, :])
        in_maps.append(m)
    return in_maps


def kernel(**inputs):
    in_maps = _host_inputs(**inputs)
    if "nc" not in _NC_CACHE:
        _NC_CACHE["nc"] = build_nc()
    nc = _NC_CACHE["nc"]
    res = run_bass_kernel_spmd(nc, in_maps, core_ids=list(range(8)))
    outp = np.empty((2, 8192, 1024), np.float32)
    for core in range(8):
        b, qr = core // 4, core % 4
        outp[b, 2048 * qr:2048 * (qr + 1)] = res.results[core]["out"]
    return outp
```

```python
import os
import math
import numpy as np
import concourse.bass as bass
import concourse.mybir as mybir
from concourse.bass_utils import run_bass_kernel_spmd

F32 = mybir.dt.float32
BF16 = mybir.dt.bfloat16
AF = mybir.ActivationFunctionType
ALU = mybir.AluOpType
AX = mybir.AxisListType

ALPHA = 2.0 ** 0.25
EPS = 1e-5
LAMBDA_INIT = 0.8 - 0.6 * math.exp(0.0)
NEG = -30000.0


class _Op:
    __slots__ = ("eng", "meth", "args", "kw", "deps", "needs_inc", "is_dma", "sem", "val")

    def __init__(self, eng, meth, args, kw, is_dma=False):
        self.eng = eng
        self.meth = meth
        self.args = args
        self.kw = kw
        self.deps = []
        self.needs_inc = False
        self.is_dma = is_dma
        self.sem = None
        self.val = None


class Prog:
    COMPUTE = ("pe", "act", "dve", "pool")

    def __init__(self, nc):
        self.nc = nc
        self.engs = {"pe": nc.tensor, "act": nc.scalar, "dve": nc.vector,
                     "pool": nc.gpsimd, "sp": nc.sync}
        self.q = {k: [] for k in self.engs}
        self.lastw = {}
        self.readers = {}
        self.last_on = {}
        self.esem = {k: nc.alloc_semaphore(name=f"es_{k}") for k in self.COMPUTE}
        self.dsem = {}
        self.dcount = {}
        self.out_dmas = []

    def _track(self, op, reads, writes):
        deps = op.deps
        for k in reads:
            w = self.lastw.get(k)
            if w is not None:
                deps.append(w)
        for k in writes:
            w = self.lastw.get(k)
            if w is not None:
                deps.append(w)
            deps.extend(self.readers.get(k, ()))
        for k in reads:
            self.readers.setdefault(k, []).append(op)
        for k in writes:
            self.lastw[k] = op
            self.readers[k] = []
        self.last_on[(op.eng, op.is_dma)] = op

    def op(self, eng, meth, *args, reads=(), writes=(), **kw):
        o = _Op(eng, meth, args, kw)
        self._track(o, reads, writes)
        self.q[eng].append(o)
        return o

    def dma(self, eng, out, in_, reads=(), writes=(), sem=None, is_output=False, **kw):
        if sem is None:
            sem = "d_" + str(writes[0] if writes else reads[0])
        if sem not in self.dsem:
            self.dsem[sem] = self.nc.alloc_semaphore(name="ds%d" % len(self.dsem))
            self.dcount[sem] = 0
        self.dcount[sem] += 16
        kw = dict(kw)
        kw["out"] = out
        kw["in_"] = in_
        o = _Op(eng, "dma_start", (), kw, is_dma=True)
        o.sem = self.dsem[sem]
        o.val = self.dcount[sem]
        self._track(o, reads, writes)
        self.q[eng].append(o)
        if is_output:
            self.out_dmas.append(o)
        return o

    def barrier(self):
        lasts = dict(self.last_on)
        for eng in self.engs:
            o = _Op(eng, None, (), {})
            for (e2, isd), l in lasts.items():
                if e2 != eng or isd:
                    o.deps.append(l)
            self.q[eng].append(o)
        return

    def _skip(self, d, o):
        return (not d.is_dma) and (not o.is_dma) and d.eng == o.eng and o.eng == "pe"

    def emit(self):
        for eng, ops in self.q.items():
            for o in ops:
                for d in o.deps:
                    if d.is_dma or self._skip(d, o):
                        continue
                    d.needs_inc = True
        for eng in self.COMPUTE:
            c = 0
            for o in self.q[eng]:
                if o.is_dma or o.meth is None:
                    continue
                if o.needs_inc:
                    c += 1
                    o.sem = self.esem[eng]
                    o.val = c
        nwaits = 0
        for eng, ops in self.q.items():
            e = self.engs[eng]
            known = {}
            for o in ops:
                need = {}
                for d in o.deps:
                    if d.sem is None or self._skip(d, o):
                        continue
                    sid = d.sem.num
                    if known.get(sid, 0) >= d.val:
                        continue
                    if sid not in need or need[sid][1] < d.val:
                        need[sid] = (d.sem, d.val)
                for sid, (s, v) in need.items():
                    e.wait_ge(s, v)
                    known[sid] = v
                    nwaits += 1
                if o.meth is None:
                    continue
                ins = getattr(e, o.meth)(*o.args, **o.kw)
                if o.is_dma:
                    ins.then_inc(o.sem, 16)
                elif o.needs_inc:
                    ins.then_inc(o.sem, 1)
        e = self.engs["sp"]
        done = {}
        for o in self.out_dmas:
            if done.get(o.sem.num, (None, 0))[1] < o.val:
                done[o.sem.num] = (o.sem, o.val)
        for sid, (s, v) in done.items():
            e.wait_ge(s, v)
        st = {k: len(v) for k, v in self.q.items()}
        st["waits"] = nwaits
        st["dsems"] = len(self.dsem)
        return st


class SBAlloc:
    def __init__(self, nc):
        self.nc = nc
        self.off = 16512
        self.top = 229344
        self.n = 0
        self.peak = 0

    def alloc(self, shape, dt):
        nb = 2 if dt == BF16 else 4
        for s in shape[1:]:
            nb *= s
        off = (self.off + 63) // 64 * 64
        self.off = off + nb
        self.peak = max(self.peak, self.off)
        assert self.off <= self.top, ("SBUF overflow", self.off, shape)
        self.n += 1
        return self.nc.alloc_sbuf_tensor_at("t%d" % self.n, list(shape), dt, offset=off)

    def mark(self):
        return self.off

    def release(self, m):
        self.off = m


def build_nc(stop=None, dbg=None):
    nc = bass.Bass("TRN2", target_bir_lowering=False)

    def din(name, shape, dt=F32):
        return nc.dram_tensor(name, list(shape), dt, kind="ExternalInput").ap()

    x_all = din("x_all", [8192, 1024])
    x_na = din("x_na", [2560, 1024])
    cT = din("cT", [128, 8])
    w_ada = din("w_ada", [1024, 6144])
    b_adaT = din("b_adaT", [128, 48])
    w_in = din("w_in", [1024, 3072])
    w_out = din("w_out", [1024, 1024])
    w_gate = din("w_gate", [1024, 2816])
    w_up = din("w_up", [1024, 2816])
    w_down = din("w_down", [2816, 1024])
    nab = din("nab", [27, 128, 1024])
    cs_all = din("cs_all", [128, 64, 64])
    cs_q = din("cs_q", [128, 16, 64])
    lamv = din("lamv", [128, 256])
    sublnB = din("sublnB", [128, 128])
    lnp = din("lnp", [128, 4096])
    identf = din("identf", [128, 128])
    out = nc.dram_tensor("out", [2048, 1024], F32, kind="ExternalOutput").ap()

    dbg_outs = []

    def dbg_out(name, ap_sb, shape, dt, key):
        if dbg is None:
            return
        d = nc.dram_tensor("dbg_" + name, list(shape), dt, kind="ExternalOutput").ap()
        dbg_outs.append((name, d, ap_sb, key))
        dbg[name] = None

    p = Prog(nc)
    sb = SBAlloc(nc)
    ps_all = nc.alloc_psum_tensor("ps_all", [128, 4096], F32)
    pb = [ps_all[:, i * 512:(i + 1) * 512] for i in range(8)]
    pbb = [t.bitcast(BF16) for t in pb]

    ident_f = sb.alloc([128, 128], F32)
    ident_b = sb.alloc([128, 128], BF16)
    ones_f = sb.alloc([128, 128], F32)
    modT = sb.alloc([128, 48], F32)
    small = sb.alloc([128, 64], F32)
    gB = sb.alloc([128, 128], F32)
    out_b = nc.alloc_sbuf_tensor_at("out_b", [128, 16, 512], BF16, offset=229344 - 16384)
    nlam = small[:, 0:1]

    p.dma("sp", ident_f[:], identf, writes=["ident_f"])
    p.op("dve", "tensor_copy", out=ident_b[:], in_=ident_f[:], reads=["ident_f"], writes=["ident_b"])
    p.op("pool", "memset", ones_f[:], 1.0, writes=["ones_f"])

    m0 = sb.mark()
    lam_sb = sb.alloc([128, 4, 64], F32)
    lam_t = sb.alloc([128, 2, 64], F32)
    p.dma("sp", lam_sb[:].rearrange("p a b -> p (a b)"), lamv, writes=["lam_sb"])
    p.dma("sp", gB[:], sublnB, writes=["gB0"])
    p.op("dve", "tensor_tensor", out=lam_t[:, 0, :], in0=lam_sb[:, 0, :], in1=lam_sb[:, 1, :], op=ALU.mult,
         reads=["lam_sb"], writes=["lam_t0"])
    p.op("dve", "tensor_tensor", out=lam_t[:, 1, :], in0=lam_sb[:, 2, :], in1=lam_sb[:, 3, :], op=ALU.mult,
         reads=["lam_sb"], writes=["lam_t1"])
    p.op("dve", "reduce_sum", out=small[:, 1:2], in_=lam_t[:, 0, :], axis=AX.X, reads=["lam_t0"], writes=["ls1"])
    p.op("dve", "reduce_sum", out=small[:, 2:3], in_=lam_t[:, 1, :], axis=AX.X, reads=["lam_t1"], writes=["ls2"])
    p.op("act", "activation", out=small[:, 3:5], in_=small[:, 1:3], func=AF.Exp, reads=["ls1", "ls2"], writes=["le"])
    p.op("dve", "tensor_tensor", out=small[:, 5:6], in0=small[:, 4:5], in1=small[:, 3:4], op=ALU.subtract,
         reads=["le"], writes=["ld"])
    p.op("dve", "tensor_scalar", out=small[:, 0:1], in0=small[:, 5:6], scalar1=-LAMBDA_INIT, scalar2=None, op0=ALU.add,
         reads=["ld"], writes=["nlam"])
    p.op("dve", "tensor_scalar", out=gB[:], in0=gB[:], scalar1=1.0 - LAMBDA_INIT, scalar2=None, op0=ALU.mult,
         reads=["gB0"], writes=["gB"])

    c_sb = sb.alloc([128, 8], F32)
    c2 = sb.alloc([128, 8, 2], BF16)
    bada = sb.alloc([128, 48], F32)
    wst = [sb.alloc([128, 8, 512], BF16) for _ in range(4)]
    p.dma("sp", c_sb[:], cT, writes=["c_sb"])
    p.dma("sp", bada[:], b_adaT, writes=["bada"])
    p.op("act", "activation", out=c2[:, :, 0], in_=c_sb[:], func=AF.Silu, reads=["c_sb"], writes=["c2a"])
    p.op("act", "activation", out=c2[:, :, 1], in_=c_sb[:], func=AF.Silu, reads=["c_sb"], writes=["c2b"])
    pmod = pb[7]
    for g in range(12):
        s = g % 4
        p.dma("pool", wst[s][:], w_ada[:, g * 512:(g + 1) * 512].rearrange("(kc p) c -> p kc c", p=128),
              writes=[("wst", s)])
        for j in range(4):
            col = g * 4 + j
            for kc in range(8):
                p.op("pe", "matmul", pmod[:, col * 2:col * 2 + 2], lhsT=wst[s][:, kc, j * 128:(j + 1) * 128],
                     rhs=c2[:, kc, :], start=(kc == 0), stop=(kc == 7), skip_group_check=True,
                     reads=[("wst", s), "c2a", "c2b"], writes=["pmod"])
    p.op("dve", "tensor_tensor", out=modT[:], in0=pmod[:, 0:96:2], in1=bada[:], op=ALU.add,
         reads=["pmod", "bada"], writes=["modT0"])
    p.op("dve", "tensor_scalar", out=modT[:, 8:24], in0=modT[:, 8:24], scalar1=1.0, scalar2=None, op0=ALU.add,
         reads=["modT0"], writes=["modT1"])
    p.op("dve", "tensor_scalar", out=modT[:, 32:48], in0=modT[:, 32:48], scalar1=1.0, scalar2=None, op0=ALU.add,
         reads=["modT0", "modT1"], writes=["modT"])
    MOD = ["modT"]
    dbg_out("modT", modT[:], [128, 48], F32, "modT")
    dbg_out("nlam", small[:, 0:8], [128, 8], F32, "nlam")

    def finish():
        for (name, d, ap_sb, key) in dbg_outs:
            p.dma("sp", d, ap_sb, reads=[key], sem="dbg_" + name, is_output=True)
        st = p.emit()
        st["sb_peak"] = sb.peak
        print("PROG", st, flush=True)
        return nc

    if stop == "A":
        p.dma("sp", out[0:128, 0:48], modT[:], reads=["modT"], sem="o", is_output=True)
        return finish()
    p.barrier()
    sb.release(m0)

    def ln_tile(tag, s, x_src, xs, zt, stt, sc_col, sh_col, tr_bank, hT_dst, hkey, x_key=None, from_sb=None):
        if from_sb is None:
            p.dma("sp", xs[:], x_src, writes=[(tag, "xs", s)])
            xin = xs
            xk = (tag, "xs", s)
        else:
            xin = from_sb
            xk = x_key
        p.op("dve", "bn_stats", out=stt[:, 0:6], in_=xin[:, 0:512], reads=[xk], writes=[(tag, "st0", s)])
        p.op("dve", "bn_stats", out=stt[:, 6:12], in_=xin[:, 512:1024], reads=[xk], writes=[(tag, "st1", s)])
        p.op("dve", "bn_aggr", out=stt[:, 12:14], in_=stt[:, 0:12], reads=[(tag, "st0", s), (tag, "st1", s)],
             writes=[(tag, "mv", s)])
        p.op("act", "activation", out=stt[:, 14:15], in_=stt[:, 13:14], func=AF.Sqrt, bias=EPS,
             reads=[(tag, "mv", s)], writes=[(tag, "sd", s)])
        p.op("dve", "reciprocal", out=stt[:, 15:16], in_=stt[:, 14:15], reads=[(tag, "sd", s)], writes=[(tag, "rstd", s)])
        p.op("dve", "scalar_tensor_tensor", out=stt[:, 16:17], in0=stt[:, 12:13], scalar=-1.0, in1=stt[:, 15:16],
             op0=ALU.mult, op1=ALU.mult, reads=[(tag, "mv", s), (tag, "rstd", s)], writes=[(tag, "nb", s)])
        p.op("act", "activation", out=zt[:], in_=xin[:], func=AF.Identity, scale=stt[:, 15:16], bias=stt[:, 16:17],
             reads=[xk, (tag, "rstd", s), (tag, "nb", s)], writes=[(tag, "zt", s)])
        trk = ("psum", tr_bank)
        for fc in range(8):
            p.op("pe", "transpose", out=pbb[tr_bank][:, fc * 128:(fc + 1) * 128], in_=zt[:, fc * 128:(fc + 1) * 128],
                 identity=ident_b[:], reads=[(tag, "zt", s), "ident_b"], writes=[trk])
        for fc in range(8):
            src = pbb[tr_bank][:, fc * 128:(fc + 1) * 128]
            if fc % 2 == 0:
                p.op("dve", "tensor_scalar", out=hT_dst(fc), in0=src, scalar1=modT[:, sc_col + fc:sc_col + fc + 1],
                     scalar2=modT[:, sh_col + fc:sh_col + fc + 1], op0=ALU.mult, op1=ALU.add,
                     reads=[trk] + MOD, writes=[hkey])
            else:
                p.op("act", "activation", out=hT_dst(fc), in_=src, func=AF.Identity,
                     scale=modT[:, sc_col + fc:sc_col + fc + 1], bias=modT[:, sh_col + fc:sh_col + fc + 1],
                     reads=[trk] + MOD, writes=[hkey])

    def run_pipeline(T, stages, order=None):
        n = len(stages)
        if order is None:
            order = list(range(n - 1, -1, -1))
        for step in range(T + n - 1):
            for k in order:
                t = step - k
                if 0 <= t < T:
                    stages[k](t)

    def hk8(base):
        return [(base, fc) for fc in range(8)]

    class LNPipe:
        def __init__(self, tag, nx, xsrc, sc_col, sh_col, tr_banks, hT_dst, hkey, xin=None):
            self.tag, self.nx, self.xsrc, self.xin = tag, nx, xsrc, xin
            self.sc, self.sh, self.trb, self.hT_dst, self.hkey = sc_col, sh_col, tr_banks, hT_dst, hkey
            self.xs = [sb.alloc([128, 1024], F32) for _ in range(nx)] if xsrc is not None else None
            self.stt = [sb.alloc([128, 32], F32) for _ in range(4)]
            self.zt = [sb.alloc([128, 1024], BF16) for _ in range(2)]

        def x(self, t):
            if self.xsrc is not None:
                return self.xs[t % self.nx], (self.tag, "xs", t % self.nx)
            return self.xin(t)

        def s_load(self, t):
            if self.xsrc is not None:
                xt, xk = self.x(t)
                p.dma("sp", xt[:], self.xsrc(t), writes=[xk])

        def s_stats(self, t):
            xt, xk = self.x(t)
            st, tg, s = self.stt[t % 4], self.tag, t % 4
            p.op("dve", "bn_stats", out=st[:, 0:6], in_=xt[:, 0:512], reads=[xk], writes=[(tg, "st0", s)])
            p.op("dve", "bn_stats", out=st[:, 6:12], in_=xt[:, 512:1024], reads=[xk], writes=[(tg, "st1", s)])
            p.op("dve", "bn_aggr", out=st[:, 12:14], in_=st[:, 0:12], reads=[(tg, "st0", s), (tg, "st1", s)],
                 writes=[(tg, "mv", s)])

        def s_rstd(self, t):
            st, tg, s = self.stt[t % 4], self.tag, t % 4
            p.op("act", "activation", out=st[:, 14:15], in_=st[:, 13:14], func=AF.Ln, bias=EPS,
                 reads=[(tg, "mv", s)], writes=[(tg, "lnv", s)])
            p.op("act", "activation", out=st[:, 15:16], in_=st[:, 14:15], func=AF.Exp, scale=-0.5,
                 reads=[(tg, "lnv", s)], writes=[(tg, "rstd", s)])

        def s_norm(self, t):
            xt, xk = self.x(t)
            st, tg, s = self.stt[t % 4], self.tag, t % 4
            p.op("dve", "tensor_scalar", out=self.zt[t % 2][:], in0=xt[:], scalar1=st[:, 12:13], scalar2=st[:, 15:16],
                 op0=ALU.subtract, op1=ALU.mult, reads=[xk, (tg, "mv", s), (tg, "rstd", s)], writes=[(tg, "zt", t % 2)])

        def s_rn_act(self, t):
            xt, xk = self.x(t)
            st, tg, s = self.stt[t % 4], self.tag, t % 4
            p.op("act", "activation", out=st[:, 14:15], in_=st[:, 13:14], func=AF.Ln, bias=EPS,
                 reads=[(tg, "mv", s)], writes=[(tg, "lnv", s)])
            p.op("act", "activation", out=st[:, 15:16], in_=st[:, 14:15], func=AF.Exp, scale=-0.5,
                 reads=[(tg, "lnv", s)], writes=[(tg, "rstd", s)])
            p.op("act", "activation", out=st[:, 17:18], in_=st[:, 12:13], func=AF.Identity, scale=st[:, 15:16],
                 reads=[(tg, "mv", s), (tg, "rstd", s)], writes=[(tg, "mr", s)])
            p.op("act", "activation", out=st[:, 16:17], in_=st[:, 17:18], func=AF.Identity, scale=-1.0,
                 reads=[(tg, "mr", s)], writes=[(tg, "nb", s)])
            p.op("act", "activation", out=self.zt[t % 2][:], in_=xt[:], func=AF.Identity, scale=st[:, 15:16],
                 bias=st[:, 16:17], reads=[xk, (tg, "rstd", s), (tg, "nb", s)], writes=[(tg, "zt", t % 2)])

        def s_te_act(self, t):
            self.s_tr(t)
            bk = self.trb[t % 2]
            for fc in range(8):
                src = pbb[bk][:, fc * 128:(fc + 1) * 128]
                sc = modT[:, self.sc + fc:self.sc + fc + 1]
                sh = modT[:, self.sh + fc:self.sh + fc + 1]
                p.op("act", "activation", out=self.hT_dst(t, fc), in_=src, func=AF.Identity, scale=sc, bias=sh,
                     reads=[("psum", bk)] + MOD, writes=[(self.hkey(t), fc)])

        def s_tr(self, t):
            bk = self.trb[t % 2]
            z = self.zt[t % 2]
            for fc in range(8):
                p.op("pe", "transpose", out=pbb[bk][:, fc * 128:(fc + 1) * 128], in_=z[:, fc * 128:(fc + 1) * 128],
                     identity=ident_b[:], reads=[(self.tag, "zt", t % 2), "ident_b"], writes=[("psum", bk)])

        def s_evac(self, t):
            bk = self.trb[t % 2]
            for fc in range(8):
                src = pbb[bk][:, fc * 128:(fc + 1) * 128]
                sc = modT[:, self.sc + fc:self.sc + fc + 1]
                sh = modT[:, self.sh + fc:self.sh + fc + 1]
                if t % 2 == 0:
                    p.op("dve", "tensor_scalar", out=self.hT_dst(t, fc), in0=src, scalar1=sc, scalar2=sh, op0=ALU.mult,
                         op1=ALU.add, reads=[("psum", bk)] + MOD, writes=[(self.hkey(t), fc)])
                else:
                    p.op("act", "activation", out=self.hT_dst(t, fc), in_=src, func=AF.Identity, scale=sc, bias=sh,
                         reads=[("psum", bk)] + MOD, writes=[(self.hkey(t), fc)])

    KT = sb.alloc([128, 4, 8192], BF16)
    VX = sb.alloc([128, 64, 4, 130], BF16)
    QT = sb.alloc([128, 4, 2048], BF16)
    m1 = sb.mark()
    Wkv = sb.alloc([128, 8, 1024], BF16)
    Wq = sb.alloc([128, 8, 512], BF16)
    hT = [sb.alloc([128, 8, 128], BF16) for _ in range(2)]
    kf = [sb.alloc([128, 8, 2, 32], F32) for _ in range(2)]
    tA = sb.alloc([128, 8, 32], F32)
    tB = sb.alloc([128, 8, 32], F32)
    tA2 = sb.alloc([128, 8, 32], F32)
    tB2 = sb.alloc([128, 8, 32], F32)
    krot = [sb.alloc([128, 8, 2, 32], BF16) for _ in range(2)]
    cst = [sb.alloc([128, 64], F32) for _ in range(4)]

    p.dma("pool", Wkv[:], w_in[:, 2048:3072].rearrange("(kc p) c -> p kc c", p=128), writes=["Wkv"])
    p.dma("pool", Wq[:], w_in[:, 1536:2048].rearrange("(kc p) c -> p kc c", p=128), writes=["Wq"])
    p.op("pool", "memset", VX[:].rearrange("p a b c -> p (a b) c")[:, :, 128:130], 1.0, writes=["VXinit"])
    NKV = 64
    NT = 80
    lnB = LNPipe("B", 4,
                 lambda t: x_all[t * 128:(t + 1) * 128, :] if t < NKV else x_na[(t - NKV + 2) * 128:(t - NKV + 3) * 128, :],
                 8, 0, (0, 1), lambda t, fc: hT[t % 2][:, fc, :], lambda t: ("hT", t % 2))

    def b_evac(t):
        lnB.s_evac(t)
        src = cs_all[:, t, :] if t < NKV else cs_q[:, t - NKV, :]
        p.dma("sp", cst[t % 4][:], src, writes=[("cst", t % 4)])

    def b_proj(t):
        s = t % 2
        W, wk = (Wkv, "Wkv") if t < NKV else (Wq, "Wq")
        for kc in range(8):
            p.op("pe", "matmul", pb[2 + s][:, :], lhsT=hT[s][:, kc, :], rhs=W[:, kc, 0:512],
                 start=(kc == 0), stop=(kc == 7), reads=hk8(("hT", s)) + [wk], writes=[("psum", 2 + s)])
        if t < NKV:
            for kc in range(8):
                p.op("pe", "matmul", pb[4 + s][:, :], lhsT=hT[s][:, kc, :], rhs=Wkv[:, kc, 512:1024],
                     start=(kc == 0), stop=(kc == 7), reads=hk8(("hT", s)) + ["Wkv"], writes=[("psum", 4 + s)])

    def b_kvevac(t):
        s = t % 2
        p.op("act", "activation", out=kf[s][:].rearrange("p a b c -> p (a b c)"), in_=pb[2 + s][:, :], func=AF.Identity,
             reads=[("psum", 2 + s)], writes=[("kf", s)])
        if t < NKV:
            p.op("act", "activation", out=VX[:, t, :, 0:128], in_=pb[4 + s][:, :].rearrange("p (h d) -> p h d", h=4),
                 func=AF.Identity, reads=[("psum", 4 + s), "VXinit"], writes=[("VX", t)])

    def b_rope(t):
        s = t % 2
        c4 = t % 4
        x1 = kf[s][:, :, 0, :]
        x2 = kf[s][:, :, 1, :]
        cosb = cst[c4][:, 0:32].unsqueeze(1).broadcast_to([128, 8, 32])
        sinb = cst[c4][:, 32:64].unsqueeze(1).broadcast_to([128, 8, 32])
        kk, ck = ("kf", s), ("cst", c4)
        p.op("pool", "tensor_tensor", out=tA[:], in0=x1, in1=cosb, op=ALU.mult, reads=[kk, ck], writes=["tA"])
        p.op("pool", "tensor_tensor", out=tB[:], in0=x2, in1=sinb, op=ALU.mult, reads=[kk, ck], writes=["tB"])
        p.op("pool", "tensor_tensor", out=krot[s][:, :, 0, :], in0=tA[:], in1=tB[:], op=ALU.subtract,
             reads=["tA", "tB"], writes=[("krot0", s)])
        p.op("dve", "tensor_tensor", out=tA2[:], in0=x2, in1=cosb, op=ALU.mult, reads=[kk, ck], writes=["tA2"])
        p.op("dve", "tensor_tensor", out=tB2[:], in0=x1, in1=sinb, op=ALU.mult, reads=[kk, ck], writes=["tB2"])
        p.op("pool", "tensor_tensor", out=krot[s][:, :, 1, :], in0=tA2[:], in1=tB2[:], op=ALU.add,
             reads=["tA2", "tB2"], writes=[("krot1", s)])

    def b_ktr(t):
        s = t % 2
        kr = krot[s][:].rearrange("p a b c -> p (a b c)")
        for h in range(4):
            p.op("pe", "transpose", out=pbb[6 + s][:, h * 128:(h + 1) * 128], in_=kr[:, h * 128:(h + 1) * 128],
                 identity=ident_b[:], reads=[("krot0", s), ("krot1", s), "ident_b"], writes=[("psum", 6 + s)])

    def b_ktevac(t):
        s = t % 2
        if t < NKV:
            dst, dkey = KT[:, :, t * 128:(t + 1) * 128], ("KT", t)
        else:
            dst, dkey = QT[:, :, (t - NKV) * 128:(t - NKV + 1) * 128], ("QT", t - NKV)
        p.op("dve", "tensor_copy", out=dst, in_=pbb[6 + s][:, 0:512].rearrange("p (h k) -> p h k", h=4),
             reads=[("psum", 6 + s)], writes=[dkey])

    run_pipeline(NT, [lnB.s_load, lnB.s_stats, lnB.s_rstd, lnB.s_norm, lnB.s_tr, b_evac, b_proj, b_kvevac, b_rope,
                      b_ktr, b_ktevac])
    KTK = [("KT", t) for t in range(64)]
    VXK = [("VX", t) for t in range(64)]
    QTK = [("QT", t) for t in range(16)]
    dbg_out("KT", KT[:, 0, 0:1024], [128, 1024], BF16, ("KT", 7))
    dbg_out("QT", QT[:, 1, 0:1024], [128, 1024], BF16, ("QT", 7))
    dbg_out("VX", VX[:, 3, :, :].rearrange("p a b -> p (a b)"), [128, 520], BF16, ("VX", 3))
    if stop == "C":
        p.barrier()
        p.dma("sp", out[0:128, 0:48], modT[:], reads=["modT"], sem="o", is_output=True)
        return finish()
    p.barrier()
    sb.release(m1)
    sb.top = 229344 - 16384

    Pb = [sb.alloc([128, 2, 512], BF16) for _ in range(3)]
    Osb = [sb.alloc([128, 8, 130], F32) for _ in range(2)]
    dsm = [sb.alloc([128, 32], F32) for _ in range(2)]
    of = [sb.alloc([128, 128], F32) for _ in range(2)]
    of2a = [sb.alloc([128, 4, 128], F32) for _ in range(2)]
    sq = sb.alloc([128, 128], F32)

    WNA_OFF = 229344 - 16384 - 24576
    assert sb.off <= WNA_OFF, sb.off
    Wna = nc.alloc_sbuf_tensor_at("Wna", [128, 8, 1536], BF16, offset=WNA_OFF)
    p.dma("pool", Wna[:], w_in[:, 0:1536].rearrange("(kc p) c -> p kc c", p=128), writes=["Wna"])

    its = [(h, qb, kt) for h in range(4) for qb in range(4) for kt in range(64)]

    def acc_ap(a, lo, hi):
        return pb[4 + a // 3][:, (a % 3) * 130 + lo:(a % 3) * 130 + hi]

    def da_qk(n):
        h, qb, kt = its[n]
        sl = n % 2
        for mp in range(2):
            p.op("pe", "matmul", pb[sl * 2 + mp][:, :], lhsT=KT[mp * 64:(mp + 1) * 64, h, kt * 128:(kt + 1) * 128],
                 rhs=QT[mp * 64:(mp + 1) * 64, h, qb * 512:(qb + 1) * 512], start=True, stop=True,
                 reads=[("KT", kt)] + QTK[qb * 4:qb * 4 + 4], writes=[("psum", sl * 2 + mp)])

    def da_exp(n):
        sl = n % 2
        p3 = n % 3
        p.op("act", "activation", out=Pb[p3][:].rearrange("p a b -> p (a b)"), in_=ps_all[:, sl * 1024:(sl + 1) * 1024],
             func=AF.Exp, scale=0.125, reads=[("psum", sl * 2), ("psum", sl * 2 + 1)], writes=[("P", p3)])

    def da_pv(n):
        h, qb, kt = its[n]
        p3 = n % 3
        for a in range(8):
            mp, qt = a // 4, a % 4
            p.op("pe", "matmul", acc_ap(a, 0, 129), lhsT=Pb[p3][:, mp, qt * 128:(qt + 1) * 128],
                 rhs=VX[:, kt, h, 0:129], start=(kt == 0 and a % 3 == 0), stop=(kt == 63), skip_group_check=True,
                 reads=[("P", p3), ("VX", kt)], writes=[("acc", a // 3)])

    def da_finish(n):
        h, qb, kt = its[n]
        blk = h * 4 + qb
        s = blk % 2
        O = Osb[s]
        for bk, (a0, na) in enumerate([(0, 3), (3, 3), (6, 2)]):
            p.op("dve", "tensor_copy", out=O[:, a0:a0 + na, 0:129],
                 in_=pb[4 + bk][:, 0:na * 130].rearrange("p (a c) -> p a c", c=130)[:, :, 0:129],
                 reads=[("acc", bk)], writes=[("Osb", s, bk)])
        ok = [("Osb", s, 0), ("Osb", s, 1), ("Osb", s, 2)]
        dm = dsm[s]
        p.op("dve", "reciprocal", out=dm[:, 0:8], in_=O[:, :, 128], reads=ok, writes=[("rinv", s)])
        p.op("dve", "tensor_scalar", out=dm[:, 8:12], in0=dm[:, 4:8], scalar1=nlam, scalar2=None, op0=ALU.mult,
             reads=[("rinv", s), "nlam"], writes=[("rn2", s)])
        for qt in range(4):
            u = qt % 2
            p.op("dve", "tensor_scalar", out=of[u][:], in0=O[:, qt, 0:128], scalar1=dm[:, qt:qt + 1], scalar2=None,
                 op0=ALU.mult, reads=ok + [("rinv", s)], writes=[("of", u)])
            p.op("dve", "scalar_tensor_tensor", out=of2a[s][:, qt, :], in0=O[:, 4 + qt, 0:128], scalar=dm[:, 8 + qt:9 + qt],
                 in1=of[u][:], op0=ALU.mult, op1=ALU.add, reads=ok + [("rn2", s), ("of", u)], writes=[("of2", s, qt)])
            p.op("pool", "tensor_tensor", out=sq[:], in0=of2a[s][:, qt, :], in1=of2a[s][:, qt, :], op=ALU.mult,
                 reads=[("of2", s, qt)], writes=["sq"])
            p.op("dve", "reduce_sum", out=dm[:, 12 + qt:13 + qt], in_=sq[:], axis=AX.X, reads=["sq"],
                 writes=[("ss", s, qt)])
        p.op("dve", "tensor_scalar", out=dm[:, 16:20], in0=dm[:, 12:16], scalar1=1.0 / 128.0, scalar2=EPS,
             op0=ALU.mult, op1=ALU.add, reads=[("ss", s, qt) for qt in range(4)], writes=[("ms", s)])

    def da_finish_b(n):
        h, qb, kt = its[n]
        blk = h * 4 + qb
        s = blk % 2
        dm = dsm[s]
        p.op("act", "activation", out=dm[:, 20:24], in_=dm[:, 16:20], func=AF.Ln, reads=[("ms", s)], writes=[("lnms", s)])
        p.op("act", "activation", out=dm[:, 24:28], in_=dm[:, 20:24], func=AF.Exp, scale=-0.5,
             reads=[("lnms", s)], writes=[("rr", s)])
        for qt in range(4):
            tile = qb * 4 + qt
            p.op("dve", "scalar_tensor_tensor", out=out_b[:, tile, h * 128:(h + 1) * 128], in0=of2a[s][:, qt, :],
                 scalar=dm[:, 24 + qt:25 + qt], in1=gB[:], op0=ALU.mult, op1=ALU.mult,
                 reads=[("of2", s, qt), ("rr", s), "gB"], writes=[("out_b", tile, h)])

    N = len(its)
    pending = []
    for n in range(N + 2):
        if n >= 2:
            da_exp(n - 2)
        if n < N:
            da_qk(n)
        if n >= 2:
            da_pv(n - 2)
            if its[n - 2][2] == 63:
                da_finish(n - 2)
                pending.append((n + 14, n - 2))
        while pending and (pending[0][0] <= n or n == N + 1):
            da_finish_b(pending.pop(0)[1])
    OBK = [("out_b", t, h) for t in range(16) for h in range(4)]
    dbg_out("out_b", out_b[:, 0, :], [128, 512], BF16, ("out_b", 0, 3))
    if stop == "D":
        p.barrier()
        p.dma("sp", out[0:128, 0:48], modT[:], reads=["modT"], sem="o", is_output=True)
        return finish()
    p.barrier()
    sb.off = m0

    out_a = sb.alloc([128, 16, 512], BF16)
    qTn = sb.alloc([128, 4, 2048], BF16)
    kTn = sb.alloc([128, 4, 2560], BF16)
    vxn = sb.alloc([128, 20, 8, 66], BF16)
    EBi = sb.alloc([128, 5, 1024], BF16)
    nst = [sb.alloc([128, 1024], F32) for _ in range(2)]
    m2 = sb.mark()
    ebcnt = [0]

    def eb_unit(u):
        s_ = ebcnt[0] % 2
        ebcnt[0] += 1
        if 11 <= u < 16:
            dst = EBi[:, u - 11, :]
        else:
            dst = EBs[:, u if u < 11 else u - 5, :]
        p.dma("sp", nst[s_][:], nab[u], writes=[("nst", s_)])
        p.op("act", "activation", out=dst, in_=nst[s_][:], func=AF.Exp, reads=[("nst", s_)], writes=[("EB", u)])

    def eb_ap(u, par):
        if 11 <= u < 16:
            return EBi[:, u - 11, par * 512:(par + 1) * 512]
        return EBs[:, u if u < 11 else u - 5, par * 512:(par + 1) * 512]

    for u in range(11, 16):
        eb_unit(u)
    sb.top = WNA_OFF
    hTn = sb.alloc([128, 8, 2560], BF16)
    p.op("pool", "memset", vxn[:].rearrange("p a b c -> p (a b) c")[:, :, 64:66], 1.0, writes=["vxinit"])
    lnE = LNPipe("E", 4, lambda j: x_na[j * 128:(j + 1) * 128, :], 8, 0, (0, 1),
                 lambda j, fc: hTn[:, fc, j * 128:(j + 1) * 128], lambda j: ("hTn", j))

    def e_vproj(j):
        bk = 2 + j % 2
        for kc in range(8):
            p.op("pe", "matmul", pb[bk][:, :], lhsT=hTn[:, kc, j * 128:(j + 1) * 128], rhs=Wna[:, kc, 1024:1536],
                 start=(kc == 0), stop=(kc == 7), reads=hk8(("hTn", j)) + ["Wna"], writes=[("psum", bk)])

    def e_vevac(j):
        bk = 2 + j % 2
        p.op("dve", "tensor_copy", out=vxn[:, j, :, 0:64], in_=pb[bk][:, :].rearrange("p (h d) -> p h d", h=8),
             reads=[("psum", bk), "vxinit"], writes=[("vxn", j)])

    run_pipeline(20, [lnE.s_load, lnE.s_stats, lnE.s_rstd, lnE.s_norm, lnE.s_tr, lnE.s_evac, e_vproj, e_vevac])
    HK = [hk8(("hTn", j)) for j in range(20)]
    cnt = 0
    for pc in range(4):
        for blk in range(5):
            bk = 4 + cnt % 2
            for kc in range(8):
                p.op("pe", "matmul", pb[bk][:, :], lhsT=Wna[:, kc, 512 + pc * 128:512 + (pc + 1) * 128],
                     rhs=hTn[:, kc, blk * 512:(blk + 1) * 512], start=(kc == 0), stop=(kc == 7),
                     reads=sum(HK[blk * 4:blk * 4 + 4], []) + ["Wna"], writes=[("psum", bk)])
            eng = "act" if cnt % 2 == 0 else "dve"
            if eng == "act":
                p.op("act", "activation", out=kTn[:, pc, blk * 512:(blk + 1) * 512], in_=pb[bk][:, :], func=AF.Identity,
                     reads=[("psum", bk)], writes=[("kTn", pc, blk)])
            else:
                p.op("dve", "tensor_copy", out=kTn[:, pc, blk * 512:(blk + 1) * 512], in_=pb[bk][:, :],
                     reads=[("psum", bk)], writes=[("kTn", pc, blk)])
            cnt += 1
        for qblk in range(4):
            bk = 4 + cnt % 2
            for kc in range(8):
                p.op("pe", "matmul", pb[bk][:, :], lhsT=Wna[:, kc, pc * 128:(pc + 1) * 128],
                     rhs=hTn[:, kc, 256 + qblk * 512:256 + (qblk + 1) * 512], start=(kc == 0), stop=(kc == 7),
                     reads=sum(HK[2 + qblk * 4:2 + qblk * 4 + 4], []) + ["Wna"], writes=[("psum", bk)])
            eng = "act" if cnt % 2 == 0 else "dve"
            if eng == "act":
                p.op("act", "activation", out=qTn[:, pc, qblk * 512:(qblk + 1) * 512], in_=pb[bk][:, :], func=AF.Identity,
                     reads=[("psum", bk)], writes=[("qTn", pc, qblk)])
            else:
                p.op("dve", "tensor_copy", out=qTn[:, pc, qblk * 512:(qblk + 1) * 512], in_=pb[bk][:, :],
                     reads=[("psum", bk)], writes=[("qTn", pc, qblk)])
            cnt += 1
    dbg_out("kTn", kTn[:, 0, 0:1024], [128, 1024], BF16, ("kTn", 0, 1))
    dbg_out("qTn", qTn[:, 0, 0:1024], [128, 1024], BF16, ("qTn", 0, 1))
    p.barrier()
    sb.release(m2)
    sb.top = 229344 - 16384
    EBs = sb.alloc([128, 22, 1024], BF16)
    Pn = [[sb.alloc([128, 512], BF16) for _ in range(2)] for _ in range(2)]
    rinvn = [sb.alloc([128, 8], F32) for _ in range(2)]
    pairs = []
    order = list(range(2, 14)) + [0, 1, 14, 15]
    for oi, i in enumerate(order):
        if i == 0:
            win, ub = list(range(0, 6)), 0
        elif i == 1:
            win, ub = list(range(1, 6)), 6
        elif i <= 13:
            win, ub = list(range(i, i + 5)), 11
        elif i == 14:
            win, ub = list(range(14, 19)), 16
        else:
            win, ub = list(range(14, 20)), 21
        for jj, j in enumerate(win):
            pairs.append((i, j, ub + jj, jj == 0, jj == len(win) - 1, oi % 2))
    sp_units = list(range(0, 11)) + list(range(16, 27))

    def na_keys_k(pc, j):
        return [("kTn", pc, j // 4)]

    def na_keys_q(pc, i):
        return [("qTn", pc, i // 4)]

    def na_qk(n):
        i, j, u, first, last, osl = pairs[n]
        sl = n % 2
        for pc in range(4):
            for par in range(2):
                p.op("pe", "matmul", pb[sl * 2 + par][:, pc * 128:(pc + 1) * 128],
                     lhsT=kTn[par * 64:(par + 1) * 64, pc, j * 128:(j + 1) * 128],
                     rhs=qTn[par * 64:(par + 1) * 64, pc, i * 128:(i + 1) * 128], start=True, stop=True,
                     skip_group_check=True,
                     reads=na_keys_k(pc, j) + na_keys_q(pc, i), writes=[("psum", sl * 2 + par)])

    def na_exp(n):
        i, j, u, first, last, osl = pairs[n]
        sl = n % 2
        for par in range(2):
            p.op("act", "activation", out=Pn[sl][par][:], in_=pb[sl * 2 + par][:, :], func=AF.Exp, scale=0.125,
                 reads=[("psum", sl * 2 + par)], writes=[("Pn0", sl, par), ("Pn", sl, par)])
            p.op("dve", "tensor_tensor", out=Pn[sl][par][:], in0=Pn[sl][par][:], in1=eb_ap(u, par),
                 op=ALU.mult, reads=[("Pn0", sl, par), ("EB", u)], writes=[("Pn", sl, par)])

    def na_pv(n):
        i, j, u, first, last, osl = pairs[n]
        sl = n % 2
        ob = 4 + osl * 2
        for pc in range(4):
            for par in range(2):
                p.op("pe", "matmul", pb[ob + par][:, pc * 66:pc * 66 + 65], lhsT=Pn[sl][par][:, pc * 128:(pc + 1) * 128],
                     rhs=vxn[:, j, 2 * pc + par, 0:65], start=(first and pc == 0), stop=last, skip_group_check=True,
                     reads=[("Pn", sl, par), ("vxn", j)], writes=[("psum", ob + par)])
        if last:
            s = osl
            for par in range(2):
                p.op("dve", "reciprocal", out=rinvn[s][:, par * 4:par * 4 + 4],
                     in_=pb[ob + par][:, 0:264].rearrange("p (a c) -> p a c", c=66)[:, :, 64],
                     reads=[("psum", ob + par)], writes=[("rinvn", s, par)])
                for pc in range(4):
                    h = 2 * pc + par
                    p.op("dve", "tensor_scalar", out=out_a[:, i, h * 64:(h + 1) * 64],
                         in0=pb[ob + par][:, pc * 66:pc * 66 + 64], scalar1=rinvn[s][:, par * 4 + pc:par * 4 + pc + 1],
                         scalar2=None, op0=ALU.mult, reads=[("psum", ob + par), ("rinvn", s, par)],
                         writes=[("out_a", i, h)])

    NP = len(pairs)
    for n in range(NP + 1):
        if n % 2 == 0 and n // 2 < len(sp_units):
            eb_unit(sp_units[n // 2])
        if n < NP:
            na_qk(n)
        if n >= 1:
            na_exp(n - 1)
            na_pv(n - 1)
    OAK = [("out_a", t, h) for t in range(16) for h in range(8)]
    dbg_out("out_a", out_a[:, 0, :], [128, 512], BF16, ("out_a", 0, 7))
    dbg_out("out_a15", out_a[:, 15, :], [128, 512], BF16, ("out_a", 15, 7))
    if stop == "E":
        p.barrier()
        p.dma("sp", out[0:128, 0:48], modT[:], reads=["modT"], sem="o", is_output=True)
        return finish()
    p.barrier()
    sb.off = m0
    out_a2 = sb.alloc([128, 16, 512], BF16)
    assert out_a2[:].offset == out_a[:].offset

    x1 = sb.alloc([128, 16, 1024], F32)
    h2T = sb.alloc([128, 8, 2048], BF16)
    mF = sb.mark()
    Wo = sb.alloc([128, 8, 1024], BF16)
    lnp1 = sb.alloc([128, 2, 1024], F32)
    G1 = sb.alloc([128, 1024], F32)
    dg = sb.alloc([128, 128], F32)
    fxs = [sb.alloc([128, 1024], F32) for _ in range(3)]
    yb = [sb.alloc([128, 1024], F32) for _ in range(4)]
    cTt = [sb.alloc([128, 8, 128], BF16) for _ in range(2)]
    stR = [sb.alloc([128, 32], F32) for _ in range(4)]
    p.dma("pool", Wo[:], w_out.rearrange("(kc p) c -> p kc c", p=128), writes=["Wo"])
    p.dma("sp", lnp1[:].rearrange("p a b -> p (a b)"), lnp[:, 0:2048], writes=["lnp1"])

    def gate_bcast(g, Gt, gkey):
        for fc in range(8):
            col = (16 if g == 0 else 40) + fc
            p.op("dve", "tensor_scalar", out=dg[:], in0=ident_f[:], scalar1=modT[:, col:col + 1], scalar2=None,
                 op0=ALU.mult, reads=["ident_f"] + MOD, writes=["dg"])
            bk = fc // 4
            p.op("pe", "matmul", pb[bk][:, (fc % 4) * 128:(fc % 4 + 1) * 128], lhsT=ones_f[:], rhs=dg[:], start=True,
                 stop=True, skip_group_check=True, reads=["dg", "ones_f"], writes=[("psum", bk)])
            if fc % 4 == 3:
                p.op("dve", "tensor_copy", out=Gt[:, bk * 512:(bk + 1) * 512], in_=pb[bk][:, :],
                     reads=[("psum", bk)], writes=[(gkey, bk)])
        return [(gkey, 0), (gkey, 1)]

    G1K = gate_bcast(0, G1, "G1")

    def resid_stats(tag, y, YB, st_, s, pm_banks, Gt, gk, xres, xres_keys):
        for half in range(2):
            p.op("dve", "tensor_tensor", out=y[:, half * 512:(half + 1) * 512], in0=pb[pm_banks[half]][:, :],
                 in1=Gt[:, half * 512:(half + 1) * 512], op=ALU.mult,
                 reads=[("psum", pm_banks[half])] + gk, writes=[YB])
        p.op("dve", "scalar_tensor_tensor", out=y[:], in0=xres, scalar=ALPHA, in1=y[:], op0=ALU.mult, op1=ALU.add,
             reads=xres_keys, writes=[YB])
        p.op("dve", "bn_stats", out=st_[:, 0:6], in_=y[:, 0:512], reads=[YB], writes=[(tag, "st0", s)])
        p.op("dve", "bn_stats", out=st_[:, 6:12], in_=y[:, 512:1024], reads=[YB], writes=[(tag, "st1", s)])
        p.op("dve", "bn_aggr", out=st_[:, 12:14], in_=st_[:, 0:12], reads=[(tag, "st0", s), (tag, "st1", s)],
             writes=[(tag, "mv", s)])

    def resid_rstd(tag, st_, s):
        p.op("act", "activation", out=st_[:, 14:15], in_=st_[:, 13:14], func=AF.Ln, bias=EPS,
             reads=[(tag, "mv", s)], writes=[(tag, "lnv", s)])
        p.op("act", "activation", out=st_[:, 15:16], in_=st_[:, 14:15], func=AF.Exp, scale=-0.5,
             reads=[(tag, "lnv", s)], writes=[(tag, "rstd", s)])

    def resid_norm(tag, y, YB, st_, s):
        p.op("dve", "scalar_tensor_tensor", out=st_[:, 16:17], in0=st_[:, 12:13], scalar=-1.0, in1=st_[:, 15:16],
             op0=ALU.mult, op1=ALU.mult, reads=[(tag, "mv", s), (tag, "rstd", s)], writes=[(tag, "nb", s)])
        p.op("act", "activation", out=y[:], in_=y[:], func=AF.Identity, scale=st_[:, 15:16], bias=st_[:, 16:17],
             reads=[(tag, "rstd", s), (tag, "nb", s)], writes=[YB])

    def resid_affine(y, YB, lnt, lk, dst, dkey):
        p.op("dve", "tensor_tensor", out=y[:], in0=y[:], in1=lnt[:, 0, :], op=ALU.mult, reads=[lk], writes=[YB])
        if dst is None:
            p.op("dve", "tensor_tensor", out=y[:], in0=y[:], in1=lnt[:, 1, :], op=ALU.add, reads=[lk], writes=[YB])
        else:
            p.op("dve", "tensor_tensor", out=dst, in0=y[:], in1=lnt[:, 1, :], op=ALU.add, reads=[YB, lk], writes=[dkey])

    def f_tr(i):
        bk = i % 2
        for fc in range(8):
            src = out_a[:, i, fc * 128:(fc + 1) * 128] if fc < 4 else out_b[:, i, (fc - 4) * 128:(fc - 3) * 128]
            rk = [("out_a", i, 2 * fc), ("out_a", i, 2 * fc + 1)] if fc < 4 else [("out_b", i, fc - 4)]
            p.op("pe", "transpose", out=pbb[bk][:, fc * 128:(fc + 1) * 128], in_=src, identity=ident_b[:],
                 reads=rk + ["ident_b"], writes=[("psum", bk)])

    def f_cevac(i):
        bk = i % 2
        p.dma("sp", fxs[i % 3][:], x_na[(i + 2) * 128:(i + 3) * 128, :], writes=[("fxs", i % 3)])
        p.op("act", "activation", out=cTt[bk][:].rearrange("p a b -> p (a b)"), in_=pbb[bk][:, :], func=AF.Identity,
             reads=[("psum", bk)], writes=[("cTt", bk)])

    def f_mix(i):
        s = i % 2
        for half in range(2):
            bk = 2 + 2 * s + half
            for kc in range(8):
                p.op("pe", "matmul", pb[bk][:, :], lhsT=cTt[s][:, kc, :], rhs=Wo[:, kc, half * 512:(half + 1) * 512],
                     start=(kc == 0), stop=(kc == 7), reads=[("cTt", s), "Wo"], writes=[("psum", bk)])

    def f_y(i):
        s = i % 2
        resid_stats("R", yb[i % 4], ("yb", i % 4), stR[i % 4], i % 4, [2 + 2 * s, 3 + 2 * s], G1, G1K,
                    fxs[i % 3][:], [("fxs", i % 3)])

    def f_rn(i):
        st_, s_, y, YB = stR[i % 4], i % 4, yb[i % 4], ("yb", i % 4)
        resid_rstd("R", st_, s_)
        p.op("act", "activation", out=st_[:, 17:18], in_=st_[:, 12:13], func=AF.Identity, scale=st_[:, 15:16],
             reads=[("R", "mv", s_), ("R", "rstd", s_)], writes=[("R", "mr", s_)])
        p.op("act", "activation", out=st_[:, 16:17], in_=st_[:, 17:18], func=AF.Identity, scale=-1.0,
             reads=[("R", "mr", s_)], writes=[("R", "nb", s_)])
        p.op("act", "activation", out=y[:], in_=y[:], func=AF.Identity, scale=st_[:, 15:16], bias=st_[:, 16:17],
             reads=[("R", "rstd", s_), ("R", "nb", s_)], writes=[YB])

    def f_trc(i):
        f_tr(i)
        f_cevac(i)

    def f_aff(i):
        resid_affine(yb[i % 4], ("yb", i % 4), lnp1, "lnp1", x1[:, i, :], ("x1", i))

    lnF = LNPipe("F", 0, None, 32, 24, (6, 7), lambda i, fc: h2T[:, fc, i * 128:(i + 1) * 128], lambda i: ("h2T", i),
                 xin=lambda i: (x1[:, i, :], ("x1", i)))
    run_pipeline(16, [f_trc, f_mix, f_y, f_rn, f_aff, lnF.s_stats, lnF.s_rn_act, lnF.s_te_act],
                 order=[3, 6, 7, 0, 1, 2, 4, 5])
    p.barrier()

    sb.top = 229344
    sb.off = m0
    G2 = sb.alloc([128, 1024], F32)
    lnp2 = sb.alloc([128, 2, 1024], F32)
    dg = sb.alloc([128, 128], F32)
    assert sb.off <= x1[:].offset if False else True
    sb.off = mF
    Wd = sb.alloc([128, 22, 1024], BF16)
    actT = sb.alloc([128, 22, 512], BF16)
    wg = [sb.alloc([128, 8, 128], BF16) for _ in range(2)]
    wu = [sb.alloc([128, 8, 128], BF16) for _ in range(2)]
    sg = [sb.alloc([128, 512], F32) for _ in range(2)]
    yb = [sb.alloc([128, 1024], F32) for _ in range(3)]
    stR = [sb.alloc([128, 32], F32) for _ in range(3)]
    p.dma("sp", lnp2[:].rearrange("p a b -> p (a b)"), lnp[:, 2048:4096], writes=["lnp2"])
    G2K = gate_bcast(1, G2, "G2")
    WDK = [("Wd", 0), ("Wd", 1)]
    ycnt = 0
    for tb in range(4):
        H2K = sum([hk8(("h2T", tb * 4 + ii)) for ii in range(4)], [])
        for ffc in range(22):
            s = ffc % 2
            p.dma("pool", wg[s][:], w_gate[:, ffc * 128:(ffc + 1) * 128].rearrange("(kc p) c -> p kc c", p=128),
                  writes=[("wg", s)])
            p.dma("pool", wu[s][:], w_up[:, ffc * 128:(ffc + 1) * 128].rearrange("(kc p) c -> p kc c", p=128),
                  writes=[("wu", s)])
            if tb == 0 and ffc == 1:
                for half in range(2):
                    p.dma("pool", Wd[:, half * 11:(half + 1) * 11, :],
                          w_down[half * 1408:(half + 1) * 1408, :].rearrange("(fc p) c -> p fc c", p=128),
                          writes=[("Wd", half)])
            for kc in range(8):
                p.op("pe", "matmul", pb[4 + s][:, :], lhsT=wg[s][:, kc, :], rhs=h2T[:, kc, tb * 512:(tb + 1) * 512],
                     start=(kc == 0), stop=(kc == 7), reads=[("wg", s)] + H2K, writes=[("psum", 4 + s)])
            for kc in range(8):
                p.op("pe", "matmul", pb[6 + s][:, :], lhsT=wu[s][:, kc, :], rhs=h2T[:, kc, tb * 512:(tb + 1) * 512],
                     start=(kc == 0), stop=(kc == 7), reads=[("wu", s)] + H2K, writes=[("psum", 6 + s)])
            p.op("act", "activation", out=sg[s][:], in_=pb[4 + s][:, :], func=AF.Silu, reads=[("psum", 4 + s)],
                 writes=[("sg", s)])
            p.op("dve", "tensor_tensor", out=actT[:, ffc, :], in0=sg[s][:], in1=pb[6 + s][:, :], op=ALU.mult,
                 reads=[("sg", s), ("psum", 6 + s)], writes=[("actT", ffc)])
        AK = [("actT", f) for f in range(22)]
        for ii in range(4):
            i = tb * 4 + ii
            s = ii % 2
            for half in range(2):
                for ffc in range(22):
                    p.op("pe", "matmul", pb[2 * s + half][:, :], lhsT=actT[:, ffc, ii * 128:(ii + 1) * 128],
                         rhs=Wd[:, ffc, half * 512:(half + 1) * 512], start=(ffc == 0), stop=(ffc == 21),
                         reads=AK + WDK, writes=[("psum", 2 * s + half)])
            ys = ycnt % 3
            ycnt += 1
            YB = ("yb2", ys)
            resid_stats("R2", yb[ys], YB, stR[ys], ys, [2 * s, 2 * s + 1], G2, G2K, x1[:, i, :], [("x1", i)])
            resid_rstd("R2", stR[ys], ys)
            resid_norm("R2", yb[ys], YB, stR[ys], ys)
            resid_affine(yb[ys], YB, lnp2, "lnp2", None, None)
            p.dma("sp", out[i * 128:(i + 1) * 128, :], yb[ys][:], reads=[YB], sem="out%d" % ys, is_output=True)
    return finish()


def _na_bias_units(rpb, qr):
    rpb = np.asarray(rpb, np.float32)
    qrw = np.arange(128) // 64
    qc = np.arange(128) % 64
    krw = np.arange(128) // 64
    kc = np.arange(128) % 64
    cs = np.clip(qc - 8, 0, 48)
    colvalid = (kc[:, None] >= cs[None, :]) & (kc[:, None] < cs[None, :] + 16)
    col_rel = kc[:, None] - qc[None, :] + 15
    col_rel_c = np.clip(col_rel, 0, 30)
    units = []

    def unit(r0, kr0):
        r = r0 + qrw
        kr = kr0 + krw
        rs = np.clip(r - 4, 0, 120)
        rowvalid = (kr[:, None] >= rs[None, :]) & (kr[:, None] < rs[None, :] + 8) & (kr[:, None] >= 0) & (kr[:, None] < 128)
        row_rel = kr[:, None] - r[None, :] + 7
        row_rel_c = np.clip(row_rel, 0, 14)
        valid = rowvalid & colvalid
        g = rpb[:, row_rel_c, col_rel_c]
        g = np.where(valid[None], g, np.float32(NEG))
        g = g.reshape(4, 2, 128, 128).transpose(2, 1, 0, 3)
        return g.reshape(128, 1024)

    base = 32 * qr
    tiles_row0 = lambda j: base - 4 + 2 * j
    for j in range(0, 6):
        units.append(unit(base + 0, tiles_row0(j)))
    for j in range(1, 6):
        units.append(unit(base + 2, tiles_row0(j)))
    for jj in range(5):
        units.append(unit(64, 60 + 2 * jj))
    for j in range(14, 19):
        units.append(unit(base + 28, tiles_row0(j)))
    for j in range(14, 20):
        units.append(unit(base + 30, tiles_row0(j)))
    return np.ascontiguousarray(np.stack(units, 0), dtype=np.float32)


_NC_CACHE = {}


def _host_inputs(x, c, w_ada, b_ada, w_in, rpb, lambda_q1, lambda_k1, lambda_q2, lambda_k2,
                 subln_g, w_out, ln1_g, ln1_b, w_gate, w_up, w_down, ln2_g, ln2_b):
    f = lambda a: np.ascontiguousarray(np.asarray(a, dtype=np.float32))
    x = f(x)
    c = f(c)
    half = 32
    freqs = (1.0 / (np.float32(10000.0) ** (np.arange(half, dtype=np.float32) / np.float32(half)))).astype(np.float32)
    pos = np.arange(8192, dtype=np.float32)
    ang = (pos[:, None] * freqs[None, :]).astype(np.float32)
    cs = np.concatenate([np.cos(ang), np.sin(ang)], axis=1).astype(np.float32)
    cs_all = np.ascontiguousarray(cs.reshape(64, 128, 64).transpose(1, 0, 2))
    lamv = np.concatenate([f(lambda_q1)[0], f(lambda_k1)[0], f(lambda_q2)[0], f(lambda_k2)[0]])[None, :]
    lamv = np.ascontiguousarray(np.broadcast_to(lamv, (128, 256)))
    sublnB = np.ascontiguousarray(np.broadcast_to(f(subln_g)[0][None, :], (128, 128)))
    lnp = np.concatenate([f(ln1_g)[0], f(ln1_b)[0], f(ln2_g)[0], f(ln2_b)[0]])[None, :]
    lnp = np.ascontiguousarray(np.broadcast_to(lnp, (128, 4096)))
    identf = np.eye(128, dtype=np.float32)
    b_adaT = np.ascontiguousarray(f(b_ada)[0].reshape(48, 128).T)
    shared = dict(w_ada=f(w_ada)[0], b_adaT=b_adaT, w_in=f(w_in)[0], w_out=f(w_out)[0], w_gate=f(w_gate)[0],
                  w_up=f(w_up)[0], w_down=f(w_down)[0], cs_all=cs_all, lamv=lamv, sublnB=sublnB, lnp=lnp,
                  identf=identf)
    nabs = [_na_bias_units(f(rpb)[0], qr) for qr in range(4)]
    in_maps = []
    for core in range(8):
        b, qr = core // 4, core % 4
        xp = np.zeros((8192 + 512, 1024), np.float32)
        xp[256:256 + 8192] = x[b]
        m = dict(shared)
        m["x_all"] = x[b]
        m["x_na"] = np.ascontiguousarray(xp[2048 * qr:2048 * qr + 2560])
        m["cT"] = np.ascontiguousarray(c[b].reshape(8, 128).T)
        m["nab"] = nabs[qr]
        m["cs_q"] = np.ascontiguousarray(cs_all[:, 16 * qr:16 * qr + 16, :])
        in_maps.append(m)
    return in_maps


def kernel(**inputs):
    in_maps = _host_inputs(**inputs)
    if "nc" not in _NC_CACHE:
        _NC_CACHE["nc"] = build_nc()
    nc = _NC_CACHE["nc"]
    res = run_bass_kernel_spmd(nc, in_maps, core_ids=list(range(8)))
    outp = np.empty((2, 8192, 1024), np.float32)
    for core in range(8):
        b, qr = core // 4, core % 4
        outp[b, 2048 * qr:2048 * (qr + 1)] = res.results[core]["out"]
    return outp
```
